# Optimizing a Trainium2 kernel written in Bass

```python
import jax
import jax.numpy as jnp
from jax import lax
import numpy as np

D_MODEL = 1024
BATCH = 8
SEQ = 2048
DEPTH = 1

GRID_W = 64
CTX_LEN = 256
EPS = 1e-6
HG_HEADS = 4
HG_DIM = 128
HG_WIDTH = HG_HEADS * HG_DIM
CHUNK = 32
ATT_HEADS = 8
ATT_KV_HEADS = 2
HEAD_DIM = 64
ATT_WIDTH = ATT_HEADS * HEAD_DIM
KV_WIDTH = ATT_KV_HEADS * HEAD_DIM
WINDOW = 128
BLOCK = 128
ROPE_THETA = 10000.0
D_FF = ((8 * D_MODEL + 3 * 256 - 1) // (3 * 256)) * 256
CTX_COLS = 3 * HG_WIDTH + 2 * KV_WIDTH
IN_COLS = CTX_COLS + 2 * HG_WIDTH + ATT_WIDTH + 2 * D_MODEL

kernel_name = 'hybrid_hgrn2_swa_prefix_dit_block'


def _rmsnorm(x, w):
    xf = x.astype(jnp.float32)
    y = xf * lax.rsqrt(jnp.mean(xf * xf, axis=-1, keepdims=True) + EPS)
    return (y * w.astype(jnp.float32)).astype(x.dtype)


def _modulate(x, w, shift, scale):
    return _rmsnorm(x, w) * (1.0 + scale) + shift


def _heads(t, n_heads, head_dim):
    return t.reshape(t.shape[0], t.shape[1], n_heads, head_dim)


def _split_ctx_side(p):
    a, b, c_, d = HG_WIDTH, 2 * HG_WIDTH, 3 * HG_WIDTH, 3 * HG_WIDTH + KV_WIDTH
    return p[..., :a], p[..., a:b], p[..., b:c_], p[..., c_:d], p[..., d:CTX_COLS]


def _split_query_side(p):
    a = CTX_COLS + HG_WIDTH
    b = a + HG_WIDTH
    c_ = b + ATT_WIDTH
    return p[..., CTX_COLS:a], p[..., a:b], p[..., b:c_], p[..., c_:IN_COLS]


def _gla_chunkwise(q, k, v, log_f, s0):
    b_, s_, h_, dk = k.shape
    dv = v.shape[-1]
    n = s_ // CHUNK

    def chunks(t):
        return t.astype(jnp.float32).reshape(b_, n, CHUNK, h_, t.shape[-1]).transpose(0, 3, 1, 2, 4)

    k, v, log_f = chunks(k), chunks(v), chunks(log_f)
    cum = jnp.cumsum(log_f, axis=3)
    cum_last = cum[:, :, :, -1:, :]
    u = jnp.einsum('bhnck,bhncv->bhnkv', k * jnp.exp(cum_last - cum), v)
    decay = jnp.exp(cum_last[:, :, :, 0, :])

    def step(state, inp):
        d, un = inp
        return d[..., None] * state + un, state

    s_final, s_start = lax.scan(step, s0.astype(jnp.float32),
                                (jnp.moveaxis(decay, 2, 0), jnp.moveaxis(u, 2, 0)))
    if q is None:
        return None, s_final
    s_start = jnp.moveaxis(s_start, 0, 2)
    qd = chunks(q) * jnp.exp(cum)
    scores = jnp.einsum('bhnck,bhnsk->bhncs', qd, k * jnp.exp(-cum))
    lower_tri = jnp.tril(jnp.ones((CHUNK, CHUNK), dtype=bool))
    scores = jnp.where(lower_tri, scores, 0.0)
    o = (jnp.einsum('bhncs,bhnsv->bhncv', scores, v)
         + jnp.einsum('bhnck,bhnkv->bhncv', qd, s_start))
    o = o.transpose(0, 2, 3, 1, 4).reshape(b_, s_, h_, dv)
    return o, s_final


def _hgrn_query(q_raw):
    return _heads(jax.nn.silu(q_raw.astype(jnp.float32)) * HG_DIM ** -0.5, HG_HEADS, HG_DIM)


def _hgrn_direction(q, f_logit, inp, lb, s0, reverse):
    f = lb + (1.0 - lb) * jax.nn.sigmoid(f_logit.astype(jnp.float32))
    log_f = _heads(jnp.log(f), HG_HEADS, HG_DIM)
    k = _heads(1.0 - f, HG_HEADS, HG_DIM)
    v = _heads(inp, HG_HEADS, HG_DIM)
    if reverse:
        q = None if q is None else jnp.flip(q, 1)
        k, v, log_f = jnp.flip(k, 1), jnp.flip(v, 1), jnp.flip(log_f, 1)
    o, s = _gla_chunkwise(q, k, v, log_f, s0)
    if reverse and o is not None:
        o = jnp.flip(o, 1)
    return o, s


def _hgrn_readout(o, g_raw, norm_w, dtype):
    g = _heads(g_raw, HG_HEADS, HG_DIM).astype(jnp.float32)
    y = _rmsnorm(o, norm_w) * jax.nn.silu(g)
    return y.reshape(o.shape[0], o.shape[1], HG_WIDTH).astype(dtype)


def _rope_1d(t, pos):
    d = t.shape[-1]
    inv_freq = ROPE_THETA ** (-jnp.arange(0, d, 2, dtype=jnp.float32) / d)
    ang = pos.astype(jnp.float32)[:, None] * inv_freq[None, :]
    cos = jnp.cos(ang)[None, :, None, :]
    sin = jnp.sin(ang)[None, :, None, :]
    tf = t.astype(jnp.float32)
    t1, t2 = tf[..., : d // 2], tf[..., d // 2:]
    return jnp.concatenate([t1 * cos - t2 * sin, t1 * sin + t2 * cos], axis=-1)


def _axial_rope(t, rows, cols):
    half = t.shape[-1] // 2
    return jnp.concatenate([_rope_1d(t[..., :half], rows), _rope_1d(t[..., half:], cols)],
                           axis=-1).astype(t.dtype)


def _window_attention(q, k, v, kc, vc, sinks):
    b_, s_, hq, dh = q.shape
    hkv = k.shape[2]
    grp = hq // hkv
    nb = s_ // BLOCK
    f32 = jnp.float32
    qb = q.astype(f32).reshape(b_, nb, BLOCK, hkv, grp, dh) * dh ** -0.5

    def band(t):
        tp = jnp.pad(t.astype(f32), ((0, 0), (BLOCK, BLOCK), (0, 0), (0, 0)))
        tp = tp.reshape(b_, nb + 2, BLOCK, hkv, dh)
        return jnp.concatenate([tp[:, :-2], tp[:, 1:-1], tp[:, 2:]], axis=2)

    kw, vw = band(k), band(v)
    s_loc = jnp.einsum('bnqhgd,bnkhd->bnhgqk', qb, kw)
    qi = jnp.arange(BLOCK)[:, None]
    kj = jnp.arange(3 * BLOCK)[None, :]
    k_pos = (jnp.arange(nb)[:, None, None] - 1) * BLOCK + kj[None]
    valid = (jnp.abs(kj - BLOCK - qi) <= WINDOW)[None] & (k_pos >= 0) & (k_pos < s_)
    s_loc = jnp.where(valid[None, :, None, None], s_loc, -jnp.inf)
    s_ctx = jnp.einsum('bnqhgd,blhd->bnhgql', qb, kc.astype(f32))
    sink = jnp.broadcast_to(sinks.astype(f32).reshape(1, 1, hkv, grp, 1, 1), s_loc.shape[:-1] + (1,))
    p = jax.nn.softmax(jnp.concatenate([s_loc, s_ctx, sink], axis=-1), axis=-1)
    n_loc = 3 * BLOCK
    n_ctx = kc.shape[1]
    o = (jnp.einsum('bnhgqk,bnkhd->bnqhgd', p[..., :n_loc], vw)
         + jnp.einsum('bnhgql,blhd->bnqhgd', p[..., n_loc:n_loc + n_ctx], vc.astype(f32)))
    return o.reshape(b_, s_, hq * dh).astype(v.dtype)


def _context_attention(qc, kc, vc, sinks):
    b_, l_, hq, dh = qc.shape
    hkv = kc.shape[2]
    grp = hq // hkv
    f32 = jnp.float32
    qg = qc.astype(f32).reshape(b_, l_, hkv, grp, dh) * dh ** -0.5
    s = jnp.einsum('blhgd,bmhd->bhglm', qg, kc.astype(f32))
    sink = jnp.broadcast_to(sinks.astype(f32).reshape(1, hkv, grp, 1, 1), s.shape[:-1] + (1,))
    p = jax.nn.softmax(jnp.concatenate([s, sink], axis=-1), axis=-1)
    o = jnp.einsum('bhglm,bmhd->blhgd', p[..., :l_], vc.astype(f32))
    return o.reshape(b_, l_, hq * dh).astype(vc.dtype)


def _merge(y_hg, y_at, gates, w_bh, w_ba, w_o):
    g_hg, g_at = jnp.split(gates, 2, axis=-1)
    mixed = jax.nn.sigmoid(g_hg) * (y_hg @ w_bh) + jax.nn.sigmoid(g_at) * (y_at @ w_ba)
    return mixed @ w_o


def _swiglu(h, w_gate, w_up, w_down):
    return (jax.nn.silu(h @ w_gate) * (h @ w_up)) @ w_down


def setup_inputs(seed: int = 0) -> dict:
    key = jax.random.key(seed)
    ks = jax.random.split(key, 20)
    f32 = jnp.float32

    def nrm(k, shape, scale):
        return jax.random.normal(k, shape, f32) * scale

    return {
        'x': nrm(ks[0], (BATCH, SEQ, D_MODEL), 1.0),
        'c': nrm(ks[1], (BATCH, D_MODEL), 1.0),
        'ctx': nrm(ks[2], (BATCH, CTX_LEN, D_MODEL), 1.0),
        'c_ctx': nrm(ks[3], (D_MODEL,), 1.0),
        'w_ada': nrm(ks[4], (DEPTH, D_MODEL, 6 * D_MODEL), 0.5 * D_MODEL ** -0.5),
        'b_ada': nrm(ks[5], (DEPTH, 6 * D_MODEL), 0.02),
        'norm_mix_w': 1.0 + nrm(ks[6], (DEPTH, D_MODEL), 0.02),
        'norm_ffn_w': 1.0 + nrm(ks[7], (DEPTH, D_MODEL), 0.02),
        'w_in': nrm(ks[8], (DEPTH, D_MODEL, IN_COLS), D_MODEL ** -0.5),
        'hgrn_lb_logits': nrm(ks[9], (2, DEPTH + 1, HG_WIDTH), 0.5),
        'hgrn_norm_w': 1.0 + nrm(ks[10], (DEPTH, HG_DIM), 0.02),
        'q_norm_w': 1.0 + nrm(ks[11], (DEPTH, HEAD_DIM), 0.02),
        'k_norm_w': 1.0 + nrm(ks[12], (DEPTH, HEAD_DIM), 0.02),
        'attn_sinks': nrm(ks[13], (DEPTH, ATT_HEADS), 0.5),
        'w_branch_hgrn': nrm(ks[14], (DEPTH, HG_WIDTH, D_MODEL), HG_WIDTH ** -0.5),
        'w_branch_attn': nrm(ks[15], (DEPTH, ATT_WIDTH, D_MODEL), ATT_WIDTH ** -0.5),
        'w_out': nrm(ks[16], (DEPTH, D_MODEL, D_MODEL), D_MODEL ** -0.5),
        'w_ffn_gate': nrm(ks[17], (DEPTH, D_MODEL, D_FF), D_MODEL ** -0.5),
        'w_ffn_up': nrm(ks[18], (DEPTH, D_MODEL, D_FF), D_MODEL ** -0.5),
        'w_ffn_down': nrm(ks[19], (DEPTH, D_FF, D_MODEL), D_FF ** -0.5),
    }


def reference(x, c, ctx, c_ctx, w_ada, b_ada, norm_mix_w, norm_ffn_w, w_in, hgrn_lb_logits,
              hgrn_norm_w, q_norm_w, k_norm_w, attn_sinks, w_branch_hgrn, w_branch_attn,
              w_out, w_ffn_gate, w_ffn_up, w_ffn_down):
    n_lat = x.shape[1]
    ROWS = n_lat // GRID_W
    rows = jnp.repeat(jnp.arange(ROWS), GRID_W)
    cols = jnp.tile(jnp.arange(GRID_W), ROWS)
    lower_bounds = jnp.cumsum(jax.nn.softmax(hgrn_lb_logits.astype(jnp.float32), axis=1), axis=1)

    for layer in range(DEPTH):
        last = layer == DEPTH - 1
        mod = jax.nn.silu(c) @ w_ada[layer] + b_ada[layer]
        mod_c = jax.nn.silu(c_ctx) @ w_ada[layer] + b_ada[layer]
        sh1, sc1, g1, sh2, sc2, g2 = [m[:, None, :] for m in jnp.split(mod, 6, axis=-1)]
        csh1, csc1, cg1, csh2, csc2, cg2 = jnp.split(mod_c, 6, axis=-1)
        lb_fwd, lb_bwd = lower_bounds[0, layer], lower_bounds[1, layer]

        h = _modulate(x, norm_mix_w[layer], sh1, sc1)
        hc = _modulate(ctx, norm_mix_w[layer], csh1, csc1)
        p = h @ w_in[layer]
        pc = hc @ (w_in[layer, :, :CTX_COLS] if last else w_in[layer])

        cf_fwd, cf_bwd, c_inp, c_k, c_v = _split_ctx_side(pc)
        s0 = jnp.zeros((ctx.shape[0], HG_HEADS, HG_DIM, HG_DIM), jnp.float32)
        c_q_hg = None if last else _hgrn_query(_split_query_side(pc)[0])
        co_fwd, cs_fwd = _hgrn_direction(c_q_hg, cf_fwd, c_inp, lb_fwd, s0, False)
        co_bwd, cs_bwd = _hgrn_direction(c_q_hg, cf_bwd, c_inp, lb_bwd, s0, True)
        ck = _rmsnorm(_heads(c_k, ATT_KV_HEADS, HEAD_DIM), k_norm_w[layer])
        cv = _heads(c_v, ATT_KV_HEADS, HEAD_DIM)

        f_fwd, f_bwd, inp, k_raw, v_raw = _split_ctx_side(p)
        q_hg_raw, g_hg, q_raw, gates = _split_query_side(p)
        q_hg = _hgrn_query(q_hg_raw)
        o_fwd, _ = _hgrn_direction(q_hg, f_fwd, inp, lb_fwd, cs_fwd, False)
        o_bwd, _ = _hgrn_direction(q_hg, f_bwd, inp, lb_bwd, cs_bwd, True)
        y_hg = _hgrn_readout(o_fwd + o_bwd, g_hg, hgrn_norm_w[layer], x.dtype)
        q = _axial_rope(_rmsnorm(_heads(q_raw, ATT_HEADS, HEAD_DIM), q_norm_w[layer]), rows, cols)
        k = _axial_rope(_rmsnorm(_heads(k_raw, ATT_KV_HEADS, HEAD_DIM), k_norm_w[layer]), rows, cols)
        y_at = _window_attention(q, k, _heads(v_raw, ATT_KV_HEADS, HEAD_DIM), ck, cv, attn_sinks[layer])
        x_new = x + g1 * _merge(y_hg, y_at, gates, w_branch_hgrn[layer], w_branch_attn[layer], w_out[layer])
        x_new = x_new + g2 * _swiglu(_modulate(x_new, norm_ffn_w[layer], sh2, sc2),
                                     w_ffn_gate[layer], w_ffn_up[layer], w_ffn_down[layer])

        if not last:
            _, cg_hg, cq_raw, c_gates = _split_query_side(pc)
            cy_hg = _hgrn_readout(co_fwd + co_bwd, cg_hg, hgrn_norm_w[layer], ctx.dtype)
            cq = _rmsnorm(_heads(cq_raw, ATT_HEADS, HEAD_DIM), q_norm_w[layer])
            cy_at = _context_attention(cq, ck, cv, attn_sinks[layer])
            ctx = ctx + cg1 * _merge(cy_hg, cy_at, c_gates, w_branch_hgrn[layer],
                                     w_branch_attn[layer], w_out[layer])
            ctx = ctx + cg2 * _swiglu(_modulate(ctx, norm_ffn_w[layer], csh2, csc2),
                                      w_ffn_gate[layer], w_ffn_up[layer], w_ffn_down[layer])
        x = x_new
    return x
```

```python
import numpy as np
import concourse.bass as bass
import concourse.mybir as mybir

F32 = mybir.dt.float32
BF16 = mybir.dt.bfloat16
AF = mybir.ActivationFunctionType
ALU = mybir.AluOpType
AX = mybir.AxisListType

ENGS = ['pe', 'act', 'dve', 'pool', 'sp']
ECODE = {e: i for i, e in enumerate(ENGS)}


def dsize(dt):
    return 4 if dt == F32 else 2


class Op:
    __slots__ = ('waits', 'fn', 'inc', 'dmakey')

    def __init__(self, fn):
        self.waits = []
        self.fn = fn
        self.inc = False
        self.dmakey = None


class Sched:
    BS = 64

    def __init__(self, nc, spaces):
        self.nc = nc
        self.ops = {e: [] for e in ENGS}
        self.seen = {e: {} for e in ENGS}
        self.seenD = {e: {} for e in ENGS}
        self.st = {}
        for name, (nbytes, bs) in spaces.items():
            n = (nbytes + bs - 1) // bs
            self.st[name] = dict(
                bs=bs,
                lw_eng=np.full(n, -1, np.int64), lw_idx=np.full(n, -1, np.int64),
                lw_dma=np.full(n, -1, np.int64),
                rd=np.full((len(ENGS), n), -1, np.int64),
                rd_dma={},
            )
        self.dmaevs = []
        self.dmacount = {}
        self.keys = {}

    def blocks(self, ap):
        name = ap.tensor.name
        st = self.st.get(name)
        if st is None:
            return None
        bs = st['bs']
        dsz = dsize(ap.dtype)
        dims = list(ap.ap)
        rowlen = ap.tensor.shape[-1]
        off = ap.offset % rowlen
        fd = list(dims[1:])
        inner = 1
        if fd and fd[-1][0] == 1:
            inner = fd[-1][1]
            fd = fd[:-1]
        starts = np.array([off], dtype=np.int64)
        for (s, c) in fd:
            if s == 0 or c == 1:
                continue
            starts = (starts[:, None] + (np.arange(c, dtype=np.int64) * s)[None, :]).ravel()
        b0 = (starts * dsz) // bs
        b1 = ((starts + inner) * dsz - 1) // bs
        if len(starts) == 1:
            idx = np.arange(b0[0], b1[0] + 1)
        else:
            span = int((b1 - b0).max()) + 1
            idx = (b0[:, None] + np.arange(span)[None, :])
            idx = idx[idx <= b1[:, None]]
            idx = np.unique(idx)
        return (name, idx)

    def _add(self, E, fn, reads, writes, rkeys=(), wkeys=(), dmakey=None):
        op = Op(fn)
        myidx = len(self.ops[E])
        ec = ECODE[E]
        deps_c = {}
        deps_d = set()
        R = [b for b in (self.blocks(a) for a in reads) if b is not None]
        W = [b for b in (self.blocks(a) for a in writes) if b is not None]

        def writers(st, idx):
            e = st['lw_eng'][idx]
            i = st['lw_idx'][idx]
            for code in np.unique(e):
                if code < 0:
                    continue
                m = int(i[e == code].max())
                if m > deps_c.get(int(code), -1):
                    deps_c[int(code)] = m
            d = st['lw_dma'][idx]
            for v in np.unique(d):
                if v >= 0:
                    deps_d.add(int(v))

        for (sp, idx) in R:
            writers(self.st[sp], idx)
            if sp == "psum":
                r = self.st[sp]['rd'][:, idx].max(axis=1)
                for code in range(len(ENGS)):
                    if r[code] >= 0 and code != ec:
                        if r[code] > deps_c.get(code, -1):
                            deps_c[code] = int(r[code])
        for (sp, idx) in W:
            st = self.st[sp]
            writers(st, idx)
            r = st['rd'][:, idx].max(axis=1)
            for code in range(len(ENGS)):
                if r[code] >= 0 and (code != ec or E != 'pe') and r[code] < (myidx if code == ec else 1 << 60):
                    if r[code] > deps_c.get(code, -1):
                        deps_c[code] = int(r[code])
            if st['rd_dma']:
                for b in idx:
                    l = st['rd_dma'].get(int(b))
                    if l:
                        deps_d.update(l)
        for k in rkeys:
            ks = self.keys.get(k)
            if ks and ks['lw'] is not None:
                deps_d.add(ks['lw'])
        for k in wkeys:
            ks = self.keys.get(k)
            if ks:
                if ks['lw'] is not None:
                    deps_d.add(ks['lw'])
                deps_d.update(ks['rd'])

        for code, idx in sorted(deps_c.items()):
            Wn = ENGS[code]
            if Wn == 'pe' and E == 'pe':
                continue
            if idx <= self.seen[E].get(Wn, -1):
                continue
            op.waits.append(('c', Wn, idx))
            self.seen[E][Wn] = idx
            self.ops[Wn][idx].inc = True
        for evid in sorted(deps_d):
            sk, val = self.dmaevs[evid]
            if val <= self.seenD[E].get(sk, 0):
                continue
            op.waits.append(('d', sk, val))
            self.seenD[E][sk] = val

        evid = None
        if dmakey is not None:
            self.dmacount[dmakey] = self.dmacount.get(dmakey, 0) + 1
            evid = len(self.dmaevs)
            self.dmaevs.append((dmakey, 16 * self.dmacount[dmakey]))
            op.dmakey = dmakey
        for (sp, idx) in R:
            st = self.st[sp]
            if evid is None:
                st['rd'][ec, idx] = myidx
            else:
                for b in idx:
                    st['rd_dma'].setdefault(int(b), []).append(evid)
        for (sp, idx) in W:
            st = self.st[sp]
            if evid is None:
                st['lw_eng'][idx] = ec
                st['lw_idx'][idx] = myidx
                st['lw_dma'][idx] = -1
            else:
                st['lw_eng'][idx] = -1
                st['lw_idx'][idx] = -1
                st['lw_dma'][idx] = evid
            st['rd'][:, idx] = -1
            if st['rd_dma']:
                for b in idx:
                    st['rd_dma'].pop(int(b), None)
        if evid is not None:
            for k in rkeys:
                self.keys.setdefault(k, dict(lw=None, rd=[]))['rd'].append(evid)
            for k in wkeys:
                self.keys[k] = dict(lw=evid, rd=[])
        self.ops[E].append(op)
        return op

    def mm(self, out, lhsT, rhs, start=True, stop=True, **kw):
        return self._add('pe', lambda e: e.matmul(out, lhsT, rhs, start=start, stop=stop, **kw),
                         [lhsT, rhs], [out])

    def tr(self, out, in_, ident):
        return self._add('pe', lambda e: e.transpose(out, in_, ident), [in_, ident], [out])

    def act(self, out, in_, func, bias=None, scale=None, accum_out=None):
        kw = {}
        reads = [in_]
        writes = [out]
        if bias is not None:
            kw['bias'] = bias
            if not isinstance(bias, (int, float)):
                reads.append(bias)
        if scale is not None:
            kw['scale'] = scale
            if not isinstance(scale, (int, float)):
                reads.append(scale)
        if accum_out is not None:
            kw['accum_out'] = accum_out
            writes.append(accum_out)
        return self._add('act', lambda e: e.activation(out=out, in_=in_, func=func, **kw), reads, writes)

    def tt(self, eng, out, in0, in1, op):
        return self._add(eng, lambda e: e.tensor_tensor(out=out, in0=in0, in1=in1, op=op), [in0, in1], [out])

    def ts(self, eng, out, in0, s1, s2, op0, op1=None, accum_out=None):
        reads = [in0]
        for s in (s1, s2):
            if s is not None and not isinstance(s, (int, float)):
                reads.append(s)
        writes = [out]
        kw = {}
        if op1 is not None:
            kw['op1'] = op1
        if accum_out is not None:
            kw['accum_out'] = accum_out
            writes.append(accum_out)
        return self._add(eng, lambda e: e.tensor_scalar(out=out, in0=in0, scalar1=s1, scalar2=s2, op0=op0, **kw),
                         reads, writes)

    def stt(self, eng, out, in0, scalar, in1, op0, op1):
        reads = [in0, in1]
        if not isinstance(scalar, (int, float)):
            reads.append(scalar)
        return self._add(eng, lambda e: e.scalar_tensor_tensor(out=out, in0=in0, scalar=scalar, in1=in1,
                                                              op0=op0, op1=op1), reads, [out])

    def copy(self, eng, out, in_):
        if eng == 'act':
            return self._add('act', lambda e: e.copy(out=out, in_=in_), [in_], [out])
        return self._add(eng, lambda e: e.tensor_copy(out=out, in_=in_), [in_], [out])

    def reduce(self, eng, out, in_, op, axis=AX.X):
        return self._add(eng, lambda e: e.tensor_reduce(out=out, in_=in_, axis=axis, op=op), [in_], [out])

    def recip(self, out, in_):
        return self._add('dve', lambda e: e.reciprocal(out=out, in_=in_), [in_], [out])

    def memset(self, eng, out, val):
        return self._add(eng, lambda e: e.memset(out, val), [], [out])

    def dma(self, q, out, in_, key, rkeys=(), wkeys=()):
        return self._add(q, lambda e: e.dma_start(out=out, in_=in_), [in_], [out],
                         rkeys=rkeys, wkeys=wkeys, dmakey=key)

    def finish(self, q='sp'):
        op = Op(None)
        for sk, cnt in self.dmacount.items():
            val = 16 * cnt
            if val > self.seenD[q].get(sk, 0):
                op.waits.append(('d', sk, val))
                self.seenD[q][sk] = val
        self.ops[q].append(op)

    def emit(self):
        nc = self.nc
        sems = {e: nc.alloc_semaphore("s_" + e) for e in ENGS}
        dsems = {k: nc.alloc_semaphore("d_" + k) for k in self.dmacount}
        rank = {}
        for e in ENGS:
            r = 0
            rk = []
            for op in self.ops[e]:
                if op.inc:
                    r += 1
                rk.append(r)
            rank[e] = rk

        def run(e, eng):
            for op in self.ops[e]:
                for w in op.waits:
                    if w[0] == 'c':
                        eng.wait_ge(sems[w[1]], rank[w[1]][w[2]])
                    else:
                        eng.wait_ge(dsems[w[1]], w[2])
                if op.fn is None:
                    continue
                ins = op.fn(eng)
                if op.dmakey is not None:
                    ins.then_inc(dsems[op.dmakey], 16)
                elif op.inc:
                    ins.then_inc(sems[e], 1)

        with nc.Block() as block:
            block.tensor(lambda t: run('pe', t))
            block.scalar(lambda t: run('act', t))
            block.vector(lambda t: run('dve', t))
            block.gpsimd(lambda t: run('pool', t))
            block.sync(lambda t: run('sp', t))
        return {e: len(self.ops[e]) for e in ENGS}


class Arena:
    def __init__(self, nc, name, nbytes):
        self.name = name
        self.nbytes = nbytes
        self.t32 = nc.alloc_sbuf_tensor(name, [128, nbytes // 4], F32)
        self.t16 = self.t32.bitcast(BF16)
        self.free = [(0, nbytes)]
        self.peak = 0

    def alloc(self, nbytes):
        nbytes = (nbytes + 63) // 64 * 64
        if nbytes >= 8192:
            for i in range(len(self.free) - 1, -1, -1):
                o, n = self.free[i]
                if n >= nbytes:
                    if n == nbytes:
                        self.free.pop(i)
                    else:
                        self.free[i] = (o, n - nbytes)
                    self.peak = max(self.peak, o + n)
                    return o + n - nbytes, nbytes
        else:
            for i, (o, n) in enumerate(self.free):
                if n >= nbytes:
                    if n == nbytes:
                        self.free.pop(i)
                    else:
                        self.free[i] = (o + nbytes, n - nbytes)
                    self.peak = max(self.peak, o + nbytes)
                    return o, nbytes
        raise RuntimeError("arena full: need %d, free=%s" % (nbytes, self.free))

    def release(self, off, nbytes):
        self.free.append((off, nbytes))
        self.free.sort()
        m = []
        for o, n in self.free:
            if m and m[-1][0] + m[-1][1] == o:
                m[-1] = (m[-1][0], m[-1][1] + n)
            else:
                m.append((o, n))
        self.free = m

    def buf(self, dtype, shape):
        return Buf(self, dtype, shape)


_L = "abcdefgh"


class Buf:
    def __init__(self, arena, dtype, shape):
        self.arena = arena
        self.dtype = dtype
        self.shape = list(shape)
        n = int(np.prod(shape[1:]))
        self.off, self.nbytes = arena.alloc(n * dsize(dtype))
        t = arena.t32 if dtype == F32 else arena.t16
        e0 = self.off // dsize(dtype)
        ap = t[0:shape[0], e0:e0 + n]
        if len(shape) > 2:
            names = [_L[i] for i in range(len(shape) - 1)]
            pat = "p (%s) -> p %s" % (" ".join(names), " ".join(names))
            ap = ap.rearrange(pat, **{names[i]: shape[i + 1] for i in range(len(names) - 1)})
        self.ap = ap

    def __getitem__(self, k):
        return self.ap[k]

    def free(self):
        self.arena.release(self.off, self.nbytes)


def mkap(ap, extra_off, dims):
    return bass.AP(ap.tensor, ap.offset + extra_off, [list(ap.ap[0])] + [list(d) for d in dims])

from concourse.bass_utils import run_bass_kernel_spmd
import ml_dtypes

P = 128
S_LEN = 2048
NT = 16
D = 1024
KC = 8
DFF = 2816
NFF = 22
EPS = 1e-6
CTX = 256
IN_COLS = 5376


def build(nc, stop=None):
    SB_BYTES = 206 * 1024
    A = Arena(nc, "arena", SB_BYTES)
    ps32 = nc.alloc_psum_tensor("psum", [128, 4096], F32)
    ps16 = ps32.bitcast(BF16)
    S = Sched(nc, {"arena": (SB_BYTES, 64), "psum": (16384, 2048)})

    def bank(b):
        return ps32[:, b * 512:(b + 1) * 512]

    def bank16(b):
        return ps16[:, b * 1024:(b + 1) * 1024]

    def din(name, shape, dt=F32):
        return nc.dram_tensor(name, list(shape), dt, kind="ExternalInput").ap()

    x = din("x", [S_LEN, D])
    ctx = din("ctx", [CTX, D])
    c2 = din("c2", [128, 16])
    w_ada = din("w_ada", [D, 6 * D])
    badaF = din("badaF", [128, 48])
    badaG = din("badaG", [128, 2048])
    nwF = din("nwF", [128, 16])
    w_in = din("w_in", [D, IN_COLS])
    lbl = din("lbl", [128, 2048])
    hnw_d = din("hnw", [128, 1])
    qkw = din("qkw", [128, 640])
    sinks = din("sinks", [128, 8])
    w_bh = din("w_bh", [512, D])
    w_ba = din("w_ba", [512, D])
    w_o = din("w_o", [D, D])
    w_g = din("w_g", [D, DFF])
    w_u = din("w_u", [D, DFF])
    w_d = din("w_d", [DFF, D])
    ident_d = din("ident", [128, 128], BF16)
    mats_d = din("mats", [128, 5 * 128])
    sel4_d = din("sel4", [128, 4])
    bmask_d = din("bmask", [128, 256], BF16)
    ropeC_d = din("ropeC", [128, 16 * 64])
    ropeS_d = din("ropeS", [128, 16 * 64])
    out = nc.dram_tensor("out", [S_LEN, D], F32, kind="ExternalOutput").ap()

    _n = [0]

    def ld(dst, src, q='sp'):
        _n[0] += 1
        S.dma(q, dst, src, "c%d" % _n[0])

    def wload(dst, src2d, key):
        if not key.startswith("f"):
            _n[0] += 1
            key = "w%d" % _n[0]
        S.dma('pool', dst, src2d.rearrange("(k p) n -> p k n", p=128), key)

    ident = A.buf(BF16, [128, 128]); ld(ident.ap, ident_d)
    mats = A.buf(F32, [128, 5, 128]); ld(mats.ap, mats_d.rearrange("p (m c) -> p m c", m=5))
    M_LE, M_GT, M_GE, M_LT, ONESD = [mats[:, i, :] for i in range(5)]
    sel4 = A.buf(F32, [128, 4]); ld(sel4.ap, sel4_d)
    matsb = A.buf(BF16, [128, 4, 128])
    S.copy('dve', matsb.ap, mats[:, 0:4, :])
    MB_LE, MB_GT, MB_GE, MB_LT = [matsb[:, i, :] for i in range(4)]
    sel4b = A.buf(BF16, [128, 4])
    S.copy('dve', sel4b.ap, sel4.ap)
    hnw = A.buf(F32, [128, 1]); ld(hnw.ap, hnw_d)
    G1B = A.buf(F32, [128, 1024]); G2B = A.buf(F32, [128, 1024])
    AB = A.buf(F32, [128, 6, 8])
    HT = A.buf(BF16, [128, 8, S_LEN])
    OB = A.buf(F32, [128, 4, S_LEN])
    Sst = {'f': A.buf(F32, [128, 512]), 'b': A.buf(F32, [128, 512])}
    NS = [[A.buf(F32, [128, 1]) for _ in range(3)] for _ in range(4)]
    NB_ = {'XS': [A.buf(F32, [128, 1024]) for _ in range(5)]}
    NB_['xsb'] = [A.buf(BF16, [128, 1024]) for _ in range(4)]
    CKT = A.buf(BF16, [128, 2, CTX])
    CV = A.buf(BF16, [128, 2, 2, 128])
    kw_ = A.buf(F32, [128, 640]); ld(kw_.ap, qkw)

    c2s = A.buf(F32, [128, 8, 2]); ld(c2s.ap, c2.rearrange("p (k t) -> p k t", k=8))
    bF = A.buf(F32, [128, 48]); ld(bF.ap, badaF)
    nws = A.buf(F32, [128, 2, 8]); ld(nws.ap, nwF.rearrange("p (a k) -> p a k", a=2))
    ld(G1B.ap, badaG[:, 0:1024]); ld(G2B.ap, badaG[:, 1024:2048])
    sg0 = A.buf(F32, [128, 8, 2])
    sc2 = A.buf(F32, [128, 8, 2])
    S.act(sg0.ap, c2s.ap, AF.Sigmoid)
    S.tt('dve', sc2.ap, c2s.ap, sg0.ap, ALU.mult)
    SCB = A.buf(BF16, [128, 8, 128])
    S.copy('dve', SCB.ap, mkap(sc2.ap, 0, [[2, 8], [0, 128]]))
    sc2b = A.buf(BF16, [128, 8, 2])
    S.copy('dve', sc2b.ap, sc2.ap)
    MODF = A.buf(F32, [128, 4, 8, 2])
    WA = [A.buf(BF16, [128, 8, 512]) for _ in range(4)]
    wa_slot = lambda cb: cb if cb < 4 else cb % 2
    tmpA = A.buf(F32, [128, 8])
    fsec = {0: 0, 1: 0, 2: 1, 3: 1, 6: 2, 7: 2, 8: 3, 9: 3}
    fbias = {0: 0, 1: 8, 2: 24, 3: 32}
    gsec = {4: (G1B, 0), 5: (G1B, 1), 10: (G2B, 0), 11: (G2B, 1)}

    def ada_load(cb):
        S.dma('pool', WA[wa_slot(cb)].ap, w_ada[:, cb * 512:(cb + 1) * 512].rearrange("(k p) n -> p k n", p=128),
              "wa%d" % wa_slot(cb))

    def ada_block(cb, pb):
        wa = WA[wa_slot(cb)]
        if cb in fsec:
            sec = fsec[cb]
            for m in range(4):
                for k in range(8):
                    S.mm(bank(pb)[:, 2 * m:2 * m + 2], wa[:, k, m * 128:(m + 1) * 128], sc2b[:, k, :],
                         start=(k == 0), stop=(k == 7))
            c0 = (cb % 2) * 4
            S.tt('dve', MODF[:, sec, c0:c0 + 4, :], bank(pb)[:, 0:8].rearrange("p (c t) -> p c t", t=2),
                 mkap(bF.ap, fbias[sec] + c0, [[1, 4], [0, 2]]), ALU.add)
        else:
            gb, half = gsec[cb]
            for k in range(8):
                S.mm(bank(pb), SCB[:, k, :], wa[:, k, :], start=(k == 0), stop=(k == 7))
            S.tt('dve', gb[:, half * 512:(half + 1) * 512], gb[:, half * 512:(half + 1) * 512], bank(pb), ALU.add)

    def ada_AB(which):
        for (ai, nwi, scsec, t) in which[0]:
            S.ts('dve', tmpA.ap, MODF[:, scsec, :, t], 1.0, None, ALU.add)
            S.tt('dve', AB[:, ai, :], tmpA.ap, nws[:, nwi, :], ALU.mult)
        for (bi, shsec, t) in which[1]:
            S.copy('dve', AB[:, bi, :], MODF[:, shsec, :, t])

    ada_order = [0, 1, 2, 3, 4, 5, 6, 7, 8, 9, 10, 11]
    for i_ in range(4):
        ada_load(i_)
    for i_ in range(4):
        ada_block(i_, i_ % 2)
    ada_load(4); ada_load(5)
    WA[2].free(); WA[3].free()
    ada_AB((((0, 0, 1, 0), (2, 0, 1, 1)), ((1, 0, 0), (3, 0, 1))))
    ada_next = [4]

    def ada_step():
        cb = ada_next[0]
        if cb >= 12:
            return
        ada_block(cb, 4)
        if cb + 2 < 12:
            ada_load(cb + 2)
        ada_next[0] += 1
        if ada_next[0] == 12:
            ada_AB((((4, 1, 3, 0),), ((5, 2, 0),)))
            for b_ in [c2s, bF, nws, sg0, sc2, sc2b, SCB, MODF, tmpA] + WA[:2]:
                b_.free()

    def norm_gen(xt, ai, bi, dst_fn, par, pb=None):
        ss, t1s, rs = NS[par]
        xsb = NB_['xsb'][par]
        if pb is None:
            pb = 6 + par
        S.memset('dve', ss.ap, 0.0); yield
        S.act(xsb.ap, xt, AF.Square, accum_out=ss.ap); yield
        S.act(t1s.ap, ss.ap, AF.Ln, bias=EPS, scale=1.0 / D); yield
        S.act(rs.ap, t1s.ap, AF.Exp, scale=-0.5); yield
        S.tt('pool', xsb.ap, xt, mkap(rs.ap, 0, [[0, 1024]]), ALU.mult); yield
        for k in range(8):
            S.tr(bank16(pb)[:, k * 128:(k + 1) * 128], xsb[:, k * 128:(k + 1) * 128], ident.ap)
        yield
        for k in range(8):
            if par % 2 == 0:
                S.act(dst_fn(k), bank16(pb)[:, k * 128:(k + 1) * 128], AF.Identity,
                      scale=AB[:, ai, k:k + 1], bias=AB[:, bi, k:k + 1])
            else:
                S.ts('dve', dst_fn(k), bank16(pb)[:, k * 128:(k + 1) * 128],
                     AB[:, ai, k:k + 1], AB[:, bi, k:k + 1], ALU.mult, ALU.add)
            yield

    def norm_to_HT(xt, ai, bi, dst_fn, pb):
        for _ in norm_gen(xt, ai, bi, dst_fn, pb - 6):
            pass

    def run_pipelined(gens, depth=2, stagger=1):
        active = []
        it = iter(gens)
        more = True
        rounds = 0
        while True:
            if more and len(active) < depth and rounds % stagger == 0:
                g = next(it, None)
                if g is None:
                    more = False
                else:
                    active.append(g)
            if not active and not more:
                break
            for g in list(active):
                if next(g, 'x') == 'x':
                    active.remove(g)
            rounds += 1

    def proj_T(b, lhs_fn, W, c0, n):
        for k in range(8):
            S.mm(bank(b)[:, 0:n], lhs_fn(k), W[:, k, c0:c0 + n], start=(k == 0), stop=(k == 7))

    OML = {'f': A.buf(F32, [128, 512]), 'b': A.buf(F32, [128, 512])}
    lbs = A.buf(F32, [128, 4, 512]); ld(lbs.ap, lbl.rearrange("p (a n) -> p a n", a=4))
    dlt = A.buf(F32, [128, 512])
    for di, d_ in enumerate(('f', 'b')):
        S.tt('dve', dlt.ap, lbs[:, 2 * di + 1, :], lbs[:, 2 * di, :], ALU.subtract)
        S.act(OML[d_].ap, dlt.ap, AF.Sigmoid)
    lbs.free(); dlt.free()
    hw = {}

    def alloc_hw():
        for nm in ('sgn', 'kk', 'lf', 'E3', 'E2', 'qs', 'osum'):
            hw[nm] = A.buf(F32, [128, 512])
        hw['E1'] = hw['sgn']
        hw['sq'] = hw['E3']
        for nm in ('qd', 'kd', 'lfh', 'lfl'):
            hw[nm] = A.buf(BF16, [128, 512])
        for nm in ('vb', 'kdec'):
            hw[nm] = [A.buf(BF16, [128, 512]) for _ in range(2)]
        hw['VM'] = [A.buf(BF16, [128, 4, 512]) for _ in range(2)]
        hw['DEC'] = [A.buf(F32, [128, 16]) for _ in range(2)]
        hw['QKT'] = [A.buf(BF16, [128, 8, 128]) for _ in range(2)]
        hw['scT'] = A.buf(BF16, [128, 4, 128])
        hw['SBs'] = [A.buf(BF16, [128, 4, 512]) for _ in range(2)]

    LNC = float(np.log(128.0 ** -0.5))

    def hgrn_proj(spec, which):
        d, lhs_fn, W, cf, ci, cq = spec[:6]
        if which == 0:
            proj_T(0, lhs_fn, W, cf, 512)
        elif which == 1:
            proj_T(1, lhs_fn, W, ci, 512)
        elif cq is not None:
            proj_T(2, lhs_fn, W, cq, 512)

    def hgrnA(spec, par, nxt):
        d, lhs_fn, W, cf, ci, cq = spec[:6]
        with_q = cq is not None
        Mcum, Mrev = (MB_LE, MB_GT) if d == 'f' else (MB_GE, MB_LT)
        vb, kdec, VM, DEC = hw['vb'][par], hw['kdec'][par], hw['VM'][par], hw['DEC'][par]
        S.act(hw['sgn'].ap, bank(0), AF.Sigmoid, scale=-1.0); yield
        if with_q:
            S.act(hw['sq'].ap, bank(2), AF.Sigmoid); yield
        S.copy('act', vb.ap, bank(1)); yield
        if nxt is not None:
            hgrn_proj(nxt, 0); yield
        S.tt('dve', hw['kk'].ap, hw['sgn'].ap, OML[d].ap, ALU.mult); yield
        if with_q:
            S.tt('dve', hw['qs'].ap, bank(2), hw['sq'].ap, ALU.mult); yield
        S.act(hw['lf'].ap, hw['kk'].ap, AF.Ln, bias=1.0, scale=-1.0); yield
        for j in range(4):
            S.tt('pool', VM[:, j, :], vb.ap, mkap(sel4.ap, j, [[0, 512]]), ALU.mult); yield
        S.copy('dve', hw['lfh'].ap, hw['lf'].ap); yield
        S.tt('dve', hw['lfl'].ap, hw['lf'].ap, hw['lfh'].ap, ALU.subtract); yield
        S.mm(bank(3), Mrev, hw['lfh'].ap, start=True, stop=False)
        S.mm(bank(3), Mrev, hw['lfl'].ap, start=False, stop=True)
        yield
        for h in range(4):
            S.mm(bank(7)[:, h * 4:(h + 1) * 4], hw['lfh'][:, h * 128:(h + 1) * 128], sel4b.ap, start=True, stop=False)
            S.mm(bank(7)[:, h * 4:(h + 1) * 4], hw['lfl'][:, h * 128:(h + 1) * 128], sel4b.ap, start=False, stop=True)
        S.act(DEC.ap, bank(7)[:, 0:16], AF.Exp); yield
        S.act(hw['E3'].ap, bank(3), AF.Exp); yield
        if nxt is not None:
            hgrn_proj(nxt, 1); yield
        S.tt('dve', kdec.ap, hw['kk'].ap, hw['E3'].ap, ALU.mult); yield
        if with_q:
            S.mm(bank(3), Mcum, hw['lfh'].ap, start=True, stop=False)
            S.mm(bank(3), Mcum, hw['lfl'].ap, start=False, stop=True); yield
            S.act(hw['E1'].ap, bank(3), AF.Exp, bias=LNC); yield
            S.act(hw['E2'].ap, bank(3), AF.Exp, scale=-1.0); yield
            if nxt is not None:
                hgrn_proj(nxt, 2); yield
            S.tt('dve', hw['qd'].ap, hw['qs'].ap, hw['E1'].ap, ALU.mult); yield
            S.tt('dve', hw['kd'].ap, hw['kk'].ap, hw['E2'].ap, ALU.mult); yield
            for h in range(4):
                S.tr(bank16(6)[:, h * 128:(h + 1) * 128], hw['qd'][:, h * 128:(h + 1) * 128], ident.ap)
            yield
            for h in range(4):
                S.tr(bank16(6)[:, (4 + h) * 128:(5 + h) * 128], hw['kd'][:, h * 128:(h + 1) * 128], ident.ap)
            yield
            S.copy('act', hw['QKT'][par].ap, bank16(6).rearrange("p (a b) -> p a b", a=8)); yield
        elif nxt is not None:
            hgrn_proj(nxt, 2); yield

    def hgrnB(d, with_q, par, out_cb):
        St = Sst[d]
        msk = M_LE if d == 'f' else M_GE
        vb, kdec, VM, DEC = hw['vb'][par], hw['kdec'][par], hw['VM'][par], hw['DEC'][par]
        QKT = hw['QKT'][par]
        SBs = hw['SBs'][par]
        if with_q:
            for h in range(4):
                S.mm(bank(7)[:, h * 128:(h + 1) * 128], QKT[:, 4 + h, :], QKT[:, h, :])
            S.tt('dve', hw['scT'].ap, bank(7).rearrange("p (a b) -> p a b", a=4),
                 mkap(msk, 0, [[0, 4], [1, 128]]), ALU.mult); yield
        order = (0, 1, 2, 3) if d == 'f' else (3, 2, 1, 0)
        for si, j in enumerate(order):
            ub = bank(4 + (si % 2))
            for h in range(4):
                S.mm(ub[:, h * 128:(h + 1) * 128], kdec[:, h * 128:(h + 1) * 128],
                     VM[:, j, h * 128:(h + 1) * 128])
            yield
            for h in range(4):
                sl = slice(h * 128, (h + 1) * 128)
                S.stt('dve', St[:, sl], St[:, sl], DEC[:, h * 4 + j:h * 4 + j + 1], ub[:, sl], ALU.mult, ALU.add)
                yield
            if si < 3:
                dst = SBs[:, order[si + 1], :]
            else:
                dst = hw['SBs'][1 - par][:, order[0], :]
            S.copy('act', dst, St.ap); yield
        if with_q:
            for h in range(4):
                S.mm(bank(5)[:, h * 128:(h + 1) * 128], vb[:, h * 128:(h + 1) * 128], hw['scT'][:, h, :],
                     start=True, stop=False, skip_group_check=True)
                for j in range(4):
                    S.mm(bank(5)[:, h * 128 + 32 * j:h * 128 + 32 * j + 32],
                         SBs[:, j, h * 128:(h + 1) * 128], QKT[:, h, 32 * j:32 * j + 32],
                         start=False, stop=(j == 3), skip_group_check=True)
                yield
            if out_cb is not None:
                for _ in out_cb(bank(5)):
                    yield

    class Pipe:
        def __init__(self):
            self.n = 0

        def run(self, specs, between=None):
            if not specs:
                return
            for w_ in range(3):
                hgrn_proj(specs[0], w_)
            prevB = None
            prevC = None
            n = len(specs)
            for i in range(n + 2):
                gens = []
                if prevC is not None:
                    gens.append((prevC(bank(5)), 8.0))
                prevC = None
                if prevB is not None:
                    gens.append((hgrnB(*prevB[0]), 38.0 if prevB[0][1] else 30.0))
                    prevC = prevB[1]
                prevB = None
                if i < n:
                    spec = specs[i]
                    par = self.n % 2
                    self.n += 1
                    nxt = specs[i + 1] if i + 1 < n else None
                    gens.append((hgrnA(spec, par, nxt), 27.0 if spec[5] is not None else 17.0))
                    prevB = ((spec[0], spec[5] is not None, par, spec[6]), spec[7])
                live = [[g_, 0, tot_] for (g_, tot_) in gens]
                while live:
                    e_ = min(live, key=lambda r: (r[1] + 1.0) / r[2])
                    if next(e_[0], 'x') == 'x':
                        live.remove(e_)
                    else:
                        e_[1] += 1
                if between is not None and i < n:
                    between(i)

        def init_snapshot(self, d):
            first = 0 if d == 'f' else 3
            S.copy('act', hw['SBs'][self.n % 2][:, first, :], Sst[d].ap)

    pipe = Pipe()

    WHG = A.buf(BF16, [128, 8, 2048])
    wload(WHG[:, :, 0:512], w_in[:, 0:512], "w")
    wload(WHG[:, :, 1536:2048], w_in[:, 512:1024], "w")
    wload(WHG[:, :, 512:1024], w_in[:, 1024:1536], "w")
    Wkv = A.buf(BF16, [128, 8, 256]); wload(Wkv.ap, w_in[:, 1536:1792], "w")
    wload(WHG[:, :, 1024:1536], w_in[:, 1792:2304], "w")
    HC = A.buf(BF16, [128, 8, CTX])
    p2tiles = list(range(NT - 1, -1, -1))

    def p2_load(n_):
        if n_ < NT:
            t_ = p2tiles[n_]
            S.dma('sp', NB_['XS'][n_ % 5].ap, x[t_ * 128:(t_ + 1) * 128, :], "xs%d" % (n_ % 5))

    def p2a_gen(n_):
        t_ = p2tiles[n_]
        for _ in norm_gen(NB_['XS'][n_ % 5].ap, 0, 1, lambda k: HT[:, k, t_ * 128:(t_ + 1) * 128], n_ % 4, pb=4 + n_ % 4):
            yield
        p2_load(n_ + 5)

    for n_ in range(5):
        p2_load(n_)
    run_pipelined((p2a_gen(n_) for n_ in range(NT)), depth=4, stagger=5)
    for t in range(2):
        xs = NB_['XS'][t % 2]
        S.dma('sp', xs.ap, ctx[t * 128:(t + 1) * 128, :], "xs%d" % (t % 2))
        norm_to_HT(xs.ap, 2, 3, lambda k, t=t: HC[:, k, t * 128:(t + 1) * 128], 6)
    for b_ in NB_['XS'] + NB_['xsb']:
        b_.free()
    alloc_hw()
    S.memset('dve', Sst['f'].ap, 0.0)
    S.memset('dve', Sst['b'].ap, 0.0)
    S.memset('dve', CV.ap, 1.0)
    ckr = A.buf(F32, [128, 128]); cks = A.buf(F32, [128, 128]); css = A.buf(F32, [128, 2])
    ct1 = A.buf(F32, [128, 2]); crs = A.buf(F32, [128, 2]); ckb = A.buf(BF16, [128, 128])
    hc_fn = lambda t: (lambda k: HC[:, k, t * 128:(t + 1) * 128])
    pipe.run([('f', hc_fn(0), WHG, 0, 512, None, None, None), ('b', hc_fn(1), WHG, 1536, 512, None, None, None),
              ('f', hc_fn(1), WHG, 0, 512, None, None, None), ('b', hc_fn(0), WHG, 1536, 512, None, None, None)])
    for t in range(2):
        lf_ = hc_fn(t)
        proj_T(5, lf_, Wkv, 0, 256)
        S.copy('act', ckr.ap, bank(5)[:, 0:128])
        S.copy('dve', CV[:, t, :, 0:64], bank(5)[:, 128:256].rearrange("p (a b) -> p a b", a=2))
        S.tt('dve', cks.ap, ckr.ap, ckr.ap, ALU.mult)
        S.reduce('dve', css.ap, cks.ap.rearrange("p (a b) -> p a b", a=2), ALU.add)
        S.act(ct1.ap, css.ap, AF.Ln, bias=EPS, scale=1.0 / 64)
        S.act(crs.ap, ct1.ap, AF.Exp, scale=-0.5)
        S.tt('dve', cks.ap.rearrange("p (a b) -> p a b", a=2), ckr.ap.rearrange("p (a b) -> p a b", a=2),
             mkap(crs.ap, 0, [[1, 2], [0, 64]]), ALU.mult)
        S.tt('dve', ckb.ap, cks.ap, kw_[:, 512:640], ALU.mult)
        for kv in range(2):
            S.tr(bank16(6)[0:64, kv * 128:(kv + 1) * 128], ckb[:, kv * 64:(kv + 1) * 64], ident.ap)
        S.copy('act', CKT[0:64, :, t * 128:(t + 1) * 128],
               bank16(6)[0:64, 0:256].rearrange("p (a b) -> p a b", a=2))
    for b_ in (ckr, cks, css, ct1, crs, ckb, HC, Wkv):
        b_.free()

    if stop == 1:
        S.finish('sp'); return S.emit(), A.peak

    def ob_store(t):
        def cb(bk):
            S.copy('dve', OB[:, :, t * 128:(t + 1) * 128], bk.rearrange("p (a b) -> p a b", a=4)); yield
        return cb

    ht_fn = lambda t: (lambda k: HT[:, k, t * 128:(t + 1) * 128])
    pipe.init_snapshot('b')
    pipe.run([('b', ht_fn(t), WHG, 1536, 512, 1024, ob_store(t), None) for t in p2tiles],
             between=lambda i: ada_step() if i % 2 == 1 else None)
    while ada_next[0] < 12:
        ada_step()

    if stop == 2:
        S.finish('sp'); return S.emit(), A.peak
    YH = A.buf(BF16, [128, 4, S_LEN])
    GS = [A.buf(BF16, [128, 2, S_LEN]) for _ in range(2)]
    wload(WHG[:, :, 1536:2048], w_in[:, 2304:2816], "w")
    sgg = A.buf(F32, [128, 512]); osq = A.buf(F32, [128, 512]); rt1 = A.buf(F32, [128, 512])
    rstd = rt1; y1 = osq

    def readout(t):
        def cb(bk):
            tg = t % 4
            S.tt('dve', hw['osum'].ap.rearrange("p (a b) -> p a b", a=4), bk.rearrange("p (a b) -> p a b", a=4),
                 OB[:, :, t * 128:(t + 1) * 128], ALU.add); yield
            S.tt('pool', osq.ap, hw['osum'].ap, hw['osum'].ap, ALU.mult); yield
            S.mm(bank(7), ONESD, osq.ap)
            S.act(rt1.ap, bank(7), AF.Ln, bias=EPS); yield
            S.act(rstd.ap, rt1.ap, AF.Exp, scale=-0.5); yield
            S.tt('dve', y1.ap, hw['osum'].ap, rstd.ap, ALU.mult); yield
            for hh in range(2):
                S.tt('pool', y1[:, hh * 256:(hh + 1) * 256].rearrange("p (a b) -> p a b", a=2),
                     y1[:, hh * 256:(hh + 1) * 256].rearrange("p (a b) -> p a b", a=2),
                     GS[hh][:, :, t * 128:(t + 1) * 128], ALU.mult)
            yield
            S.ts('dve', YH[:, :, t * 128:(t + 1) * 128], y1.ap.rearrange("p (a b) -> p a b", a=4),
                 hnw[:, 0:1], None, ALU.mult); yield
        return cb

    for g in range(4):
        g0 = g * 512
        for m in range(4):
            for k in range(8):
                S.mm(bank(4 + m), WHG[:, k, 1536 + m * 128:1536 + (m + 1) * 128], HT[:, k, g0:g0 + 512],
                     start=(k == 0), stop=(k == 7))
            S.act(sgg.ap, bank(4 + m), AF.Sigmoid)
            S.tt('dve', GS[m // 2][:, m % 2, g0:g0 + 512], bank(4 + m), sgg.ap, ALU.mult)
    pipe.init_snapshot('f')
    pipe.run([('f', ht_fn(t), WHG, 0, 512, 1024, None, readout(t)) for t in range(NT)])
    for b_ in [WHG, sgg, osq, rt1, OB] + GS:
        b_.free()
    for nm, b_ in hw.items():
        if nm in ('E1', 'sq'):
            continue
        for bb in (b_ if isinstance(b_, list) else [b_]):
            bb.free()
    OML['f'].free(); OML['b'].free()

    if stop == 3:
        S.finish('sp'); return S.emit(), A.peak
    W4 = A.buf(BF16, [128, 8, 768])
    wload(W4[:, :, 0:512], w_in[:, 2816:3328], "w0")
    wload(W4[:, :, 512:768], w_in[:, 1536:1792], "w1")
    ropeC = A.buf(F32, [128, 16, 64]); ld(ropeC.ap, ropeC_d.rearrange("p (a b) -> p a b", a=16))
    ropeS = A.buf(F32, [128, 16, 64]); ld(ropeS.ap, ropeS_d.rearrange("p (a b) -> p a b", a=16))
    bmask = A.buf(BF16, [128, 2, 128]); ld(bmask.ap, bmask_d.rearrange("p (a b) -> p a b", a=2))
    esk = A.buf(F32, [128, 8]); ld(esk.ap, sinks)
    S.act(esk.ap, esk.ap, AF.Exp)
    YA = A.buf(BF16, [128, 4, S_LEN])
    KT = A.buf(BF16, [128, 2, S_LEN])
    VA = A.buf(BF16, [128, NT, 2, 128])
    S.memset('dve', VA.ap, 1.0)
    QT = [A.buf(BF16, [128, 8, 128]) for _ in range(3)]
    qkr = A.buf(F32, [128, 640]); qsq = A.buf(F32, [128, 640]); qss = A.buf(F32, [128, 10])
    qt1 = A.buf(F32, [128, 10]); qrs = A.buf(F32, [128, 10]); qn = A.buf(F32, [128, 640])
    r1 = A.buf(F32, [128, 640]); r2 = A.buf(F32, [128, 640]); qr = A.buf(BF16, [128, 640])
    PT = [A.buf(BF16, [128, 512]) for _ in range(10)]
    den = [A.buf(F32, [128, 4]) for _ in range(2)]; rden = [A.buf(F32, [128, 4]) for _ in range(2)]
    YT = [A.buf(BF16, [128, 512]) for _ in range(2)]

    def attn_proj(t):
        lf_ = lambda k: HT[:, k, t * 128:(t + 1) * 128]
        proj_T(0, lf_, W4, 0, 512)
        proj_T(1, lf_, W4, 512, 256)

    def attn_prep(t):
        S.act(qsq[:, 0:512], bank(0), AF.Square); yield
        S.act(qsq[:, 512:640], bank(1)[:, 0:128], AF.Square); yield
        S.copy('act', VA[:, t, :, 0:64], bank(1)[:, 128:256].rearrange("p (a b) -> p a b", a=2)); yield
        S.reduce('dve', qss.ap, qsq.ap.rearrange("p (a b) -> p a b", a=10), ALU.add); yield
        S.act(qt1.ap, qss.ap, AF.Ln, bias=EPS, scale=1.0 / 64); yield
        S.act(qrs.ap, qt1.ap, AF.Exp, scale=-0.5); yield
        S.tt('dve', qn[:, 0:512].rearrange("p (a b) -> p a b", a=8), bank(0).rearrange("p (a b) -> p a b", a=8),
             mkap(qrs.ap, 0, [[1, 8], [0, 64]]), ALU.mult); yield
        S.tt('dve', qn[:, 512:640].rearrange("p (a b) -> p a b", a=2), bank(1)[:, 0:128].rearrange("p (a b) -> p a b", a=2),
             mkap(qrs.ap, 8, [[1, 2], [0, 64]]), ALU.mult); yield
        if t + 1 < NT:
            attn_proj(t + 1); yield
        S.tt('dve', qn.ap, qn.ap, kw_.ap, ALU.mult); yield
        S.tt('dve', r1.ap.rearrange("p (a b) -> p a b", a=10), qn.ap.rearrange("p (a b) -> p a b", a=10),
             mkap(ropeC.ap, t * 64, [[0, 10], [1, 64]]), ALU.mult); yield
        for hf in range(2):
            S.tt('dve', mkap(r2.ap, 16 * hf, [[64, 10], [32, 2], [1, 16]]),
                 mkap(qn.ap, 16 * (1 - hf), [[64, 10], [32, 2], [1, 16]]),
                 mkap(ropeS.ap, t * 64 + 16 * hf, [[0, 10], [32, 2], [1, 16]]), ALU.mult); yield
        S.tt('dve', qr.ap, r1.ap, r2.ap, ALU.add); yield
        for h in range(8):
            S.tr(bank16(2)[0:64, h * 128:(h + 1) * 128], qr[:, h * 64:(h + 1) * 64], ident.ap)
        yield
        for kv in range(2):
            S.tr(bank16(3)[0:64, kv * 128:(kv + 1) * 128], qr[:, 512 + kv * 64:512 + (kv + 1) * 64], ident.ap)
        yield
        S.copy('act', QT[t % 3][0:64, :, :], bank16(2)[0:64, :].rearrange("p (a b) -> p a b", a=8)); yield
        S.copy('dve', KT[0:64, :, t * 128:(t + 1) * 128], bank16(3)[0:64, 0:256].rearrange("p (a b) -> p a b", a=2)); yield

    def attn_kv(i, kv):
        kts = [('c', 0), ('c', 1)]
        if i > 0:
            kts.append(('p', i - 1))
        kts.append(('s', i))
        if i < NT - 1:
            kts.append(('n', i + 1))
        q_ = QT[i % 3]
        PTk = PT[kv * 5:(kv + 1) * 5]
        bk = bank(4 + kv)
        oa = bank(6 + kv)
        for n, (kind, ti) in enumerate(kts):
            if kind == 'c':
                lhsT = CKT[0:64, kv, ti * 128:(ti + 1) * 128]
            else:
                lhsT = KT[0:64, kv, ti * 128:(ti + 1) * 128]
            S.mm(bk, lhsT, q_[0:64, kv * 4:(kv + 1) * 4, :]); yield
            S.act(PTk[n].ap, bk, AF.Exp, scale=0.125); yield
            if kind in ('p', 'n'):
                mi = 0 if kind == 'p' else 1
                S.tt('dve', PTk[n].ap.rearrange("p (a b) -> p a b", a=4), PTk[n].ap.rearrange("p (a b) -> p a b", a=4),
                     mkap(bmask.ap, mi * 128, [[0, 4], [1, 128]]), ALU.mult); yield
        for h in range(4):
            for n, (kind, ti) in enumerate(kts):
                vsrc = CV[:, ti, kv, 0:65] if kind == 'c' else VA[:, ti, kv, 0:65]
                S.mm(oa[:, h * 128:h * 128 + 65], PTk[n][:, h * 128:(h + 1) * 128], vsrc,
                     start=(n == 0), stop=(n == len(kts) - 1))
            yield
        oav = oa.rearrange("p (a b) -> p a b", a=4)
        S.tt('dve', den[kv].ap, oav[:, :, 64], esk[:, kv * 4:(kv + 1) * 4], ALU.add); yield
        S.recip(rden[kv].ap, den[kv].ap); yield
        S.tt('dve', YT[i % 2][:, kv * 256:(kv + 1) * 256].rearrange("p (a b) -> p a b", a=4), oav[:, :, 0:64],
             mkap(rden[kv].ap, 0, [[1, 4], [0, 64]]), ALU.mult); yield

    def attn_fin(i):
        for c in range(4):
            S.tr(bank16(4)[:, c * 128:(c + 1) * 128], YT[i % 2][:, c * 128:(c + 1) * 128], ident.ap)
        S.copy('act', YA[:, :, i * 128:(i + 1) * 128], bank16(4)[:, 0:512].rearrange("p (a b) -> p a b", a=4))

    def prop_run(items):
        live = [[g_, 0, n_] for (g_, n_) in items]
        while live:
            e_ = min(live, key=lambda r: (r[1] + 1.0) / r[2])
            if next(e_[0], 'x') == 'x':
                live.remove(e_)
            else:
                e_[1] += 1

    attn_proj(0)
    for t in range(NT + 2):
        items = []
        if t >= 2:
            items.append((attn_kv(t - 2, 0), 22.0))
            items.append((attn_kv(t - 2, 1), 22.0))
        if t < NT:
            items.append((attn_prep(t), 19.0))
        prop_run(items)
        if t >= 2:
            attn_fin(t - 2)
    for b_ in [W4, ropeC, ropeS, bmask, esk, KT, VA, qkr, qsq, qss, qt1, qrs, qn, r1, r2, qr, kw_] + QT + PT + den + rden + YT:
        b_.free()

    if stop == 4:
        S.finish('sp'); return S.emit(), A.peak
    Wg = A.buf(BF16, [128, 8, 2048])
    for i_ in range(4):
        wload(Wg[:, :, i_ * 512:(i_ + 1) * 512], w_in[:, 3328 + i_ * 512:3328 + (i_ + 1) * 512], "w%d" % i_)
    Wbh = A.buf(BF16, [128, 4, 1024]); Wba = A.buf(BF16, [128, 4, 1024])
    wload(Wbh.ap, w_bh, "w")
    wload(Wba.ap, w_ba, "w")
    MT = A.buf(BF16, [128, 8, S_LEN])
    sgh = A.buf(F32, [128, 512]); sga = A.buf(F32, [128, 512]); m1 = A.buf(F32, [128, 512]); m2 = A.buf(F32, [128, 512])
    it = 0
    for g in range(4):
        g0 = g * 512
        for c in range(8):
            pb = 4 * (it % 2)
            it += 1
            for k in range(8):
                S.mm(bank(pb), Wg[:, k, c * 128:(c + 1) * 128], HT[:, k, g0:g0 + 512], start=(k == 0), stop=(k == 7))
            for k in range(8):
                S.mm(bank(pb + 1), Wg[:, k, 1024 + c * 128:1024 + (c + 1) * 128], HT[:, k, g0:g0 + 512],
                     start=(k == 0), stop=(k == 7))
            for k in range(4):
                S.mm(bank(pb + 2), Wbh[:, k, c * 128:(c + 1) * 128], YH[:, k, g0:g0 + 512], start=(k == 0), stop=(k == 3))
            for k in range(4):
                S.mm(bank(pb + 3), Wba[:, k, c * 128:(c + 1) * 128], YA[:, k, g0:g0 + 512], start=(k == 0), stop=(k == 3))
            S.act(sgh.ap, bank(pb), AF.Sigmoid)
            S.act(sga.ap, bank(pb + 1), AF.Sigmoid)
            S.tt('dve', m1.ap, sgh.ap, bank(pb + 2), ALU.mult)
            S.tt('dve', m2.ap, sga.ap, bank(pb + 3), ALU.mult)
            S.tt('dve', MT[:, c, g0:g0 + 512], m1.ap, m2.ap, ALU.add)
    for b_ in (Wg, Wbh, Wba, sgh, sga, m1, m2, YH, YA):
        b_.free()

    if stop == 5:
        S.finish('sp'); return S.emit(), A.peak
    Wo = A.buf(BF16, [128, 8, 1024])
    WG = [A.buf(BF16, [128, 8, 256]) for _ in range(2)]
    WU = [A.buf(BF16, [128, 8, 256]) for _ in range(2)]

    def ffn_wload(blk):
        wload(WG[blk % 2].ap, w_g[:, blk * 256:(blk + 1) * 256], "fg%d" % (blk % 2))
        wload(WU[blk % 2].ap, w_u[:, blk * 256:(blk + 1) * 256], "fu%d" % (blk % 2))

    P5D = 4
    NB_['XS'] = [A.buf(F32, [128, 1024]) for _ in range(P5D + 1)]
    NB_['xsb'] = [A.buf(BF16, [128, 1024]) for _ in range(P5D)]
    wload(Wo[:, :, 0:512], w_o[:, 0:512], "w0")
    wload(Wo[:, :, 512:1024], w_o[:, 512:1024], "w1")
    XN = [A.buf(F32, [128, 1024]) for _ in range(P5D)]
    ffn_wload(0); ffn_wload(1)

    def p5_load(t_):
        if t_ < NT:
            sl = t_ % (P5D + 1)
            S.dma('sp', NB_['XS'][sl].ap, x[t_ * 128:(t_ + 1) * 128, :], "xs%d" % sl)

    def p5_gen(t):
        sl = t % P5D
        xs = NB_['XS'][t % (P5D + 1)]
        xn = XN[sl]
        for hf in range(2):
            bk = bank(2 * sl + hf)
            for k in range(8):
                S.mm(bk, MT[:, k, t * 128:(t + 1) * 128], Wo[:, k, hf * 512:(hf + 1) * 512],
                     start=(k == 0), stop=(k == 7))
            yield
            S.tt('dve', xn[:, hf * 512:(hf + 1) * 512], bk, G1B[:, hf * 512:(hf + 1) * 512], ALU.mult); yield
        S.tt('dve', xn.ap, xn.ap, xs.ap, ALU.add); yield
        p5_load(t + P5D + 1)
        S.dma('sp', out[t * 128:(t + 1) * 128, :], xn.ap, "xn%d" % sl, wkeys=["o%d" % t]); yield
        for _ in norm_gen(xn.ap, 4, 5, lambda k: HT[:, k, t * 128:(t + 1) * 128], sl, pb=2 * sl):
            yield

    for t_ in range(P5D + 1):
        p5_load(t_)
    run_pipelined((p5_gen(t) for t in range(NT)), depth=P5D, stagger=5)
    for b_ in [Wo, MT] + NB_['xsb'] + XN + NB_['XS'][3:]:
        b_.free()

    if stop == 6:
        S.finish('sp'); return S.emit(), A.peak
    Wd = A.buf(BF16, [128, NFF, 1024])
    HID = A.buf(BF16, [128, NFF, 1024])
    sgf = [A.buf(F32, [128, 512]) for _ in range(2)]
    fin = [A.buf(F32, [128, 1024]) for _ in range(2)]
    it = 0
    for half in range(2):
        t0 = half * 1024
        for blk in range(11):
            wg_ = WG[blk % 2]; wu_ = WU[blk % 2]
            if not (half == 0 and blk < 2):
                ffn_wload(blk)
            if half == 0:
                wload(Wd[:, 2 * blk:2 * blk + 2, :], w_d[2 * blk * 128:(2 * blk + 2) * 128, :], "w")
            for jj in range(2):
                j = blk * 2 + jj
                for g in range(2):
                    g0 = t0 + g * 512
                    pb = 2 * (it % 3)
                    sg_ = sgf[it % 2]
                    it += 1
                    for k in range(8):
                        S.mm(bank(pb), wg_[:, k, jj * 128:(jj + 1) * 128], HT[:, k, g0:g0 + 512],
                             start=(k == 0), stop=(k == 7))
                    for k in range(8):
                        S.mm(bank(pb + 1), wu_[:, k, jj * 128:(jj + 1) * 128], HT[:, k, g0:g0 + 512],
                             start=(k == 0), stop=(k == 7))
                    S.act(sg_.ap, bank(pb), AF.Silu)
                    S.tt('dve', HID[:, j, g * 512:(g + 1) * 512], sg_.ap, bank(pb + 1), ALU.mult)
        def p6_load(t_):
            if t_ < (half + 1) * 8:
                S.dma('sp', NB_['XS'][t_ % 3].ap, out[t_ * 128:(t_ + 1) * 128, :], "xs%d" % (t_ % 3), rkeys=["o%d" % t_])

        p6_load(half * 8); p6_load(half * 8 + 1)
        for tt_ in range(8):
            t = half * 8 + tt_
            xs = NB_['XS'][t % 3]
            fo = fin[t % 2]
            p6_load(t + 2)
            for hf in range(2):
                for j in range(NFF):
                    S.mm(bank(6 + hf), HID[:, j, tt_ * 128:(tt_ + 1) * 128], Wd[:, j, hf * 512:(hf + 1) * 512],
                         start=(j == 0), stop=(j == NFF - 1))
                S.tt('dve', fo[:, hf * 512:(hf + 1) * 512], bank(6 + hf), G2B[:, hf * 512:(hf + 1) * 512], ALU.mult)
            S.tt('pool', fo.ap, fo.ap, xs.ap, ALU.add)
            S.dma('sp', out[t * 128:(t + 1) * 128, :], fo.ap, "fo%d" % (t % 2), wkeys=["o%d" % t])
    S.finish('sp')
    counts = S.emit()
    return counts, A.peak


_CONST_CACHE = {}


def _consts():
    if _CONST_CACHE:
        return _CONST_CACHE
    p = np.arange(128)
    same = (p[:, None] // 32) == (p[None, :] // 32)
    LE = (same & (p[:, None] <= p[None, :])).astype(np.float32)
    GT = (same & (p[:, None] > p[None, :])).astype(np.float32)
    GE = (same & (p[:, None] >= p[None, :])).astype(np.float32)
    LT = (same & (p[:, None] < p[None, :])).astype(np.float32)
    ONESD = np.full((128, 128), 1.0 / 128.0, np.float32)
    mats = np.concatenate([LE, GT, GE, LT, ONESD], axis=1)
    sel4 = (p[:, None] // 32 == np.arange(4)[None, :]).astype(np.float32)
    maskP = (p[None, :] <= p[:, None]).astype(np.float32)
    maskN = (p[:, None] <= p[None, :]).astype(np.float32)
    bmask = np.concatenate([maskP, maskN], axis=1).astype(ml_dtypes.bfloat16)
    t = np.arange(S_LEN)
    rows = (t // 64).astype(np.float32)
    cols = (t % 64).astype(np.float32)
    inv = (10000.0 ** (-np.arange(0, 32, 2, dtype=np.float32) / 32)).astype(np.float32)
    angR = rows[:, None] * inv[None, :]
    angC = cols[:, None] * inv[None, :]
    C = np.concatenate([np.cos(angR), np.cos(angR), np.cos(angC), np.cos(angC)], axis=1)
    Sn = np.concatenate([-np.sin(angR), np.sin(angR), -np.sin(angC), np.sin(angC)], axis=1)
    ropeC = C.reshape(16, 128, 64).transpose(1, 0, 2).reshape(128, 1024).astype(np.float32)
    ropeS = Sn.reshape(16, 128, 64).transpose(1, 0, 2).reshape(128, 1024).astype(np.float32)
    _CONST_CACHE.update(dict(
        ident=np.eye(128).astype(ml_dtypes.bfloat16), mats=np.ascontiguousarray(mats), sel4=sel4,
        bmask=np.ascontiguousarray(bmask), ropeC=np.ascontiguousarray(ropeC), ropeS=np.ascontiguousarray(ropeS)))
    return _CONST_CACHE


def _rep(v):
    return np.ascontiguousarray(np.broadcast_to(np.asarray(v, np.float32).reshape(1, -1), (128, np.asarray(v).size)))


def _pk(v):
    v = np.asarray(v, np.float32).reshape(-1, 128)
    return np.ascontiguousarray(v.T)


def kernel(x, c, ctx, c_ctx, w_ada, b_ada, norm_mix_w, norm_ffn_w, w_in, hgrn_lb_logits,
           hgrn_norm_w, q_norm_w, k_norm_w, attn_sinks, w_branch_hgrn, w_branch_attn,
           w_out, w_ffn_gate, w_ffn_up, w_ffn_down):
    f = lambda a: np.ascontiguousarray(np.asarray(a, np.float32))
    x = f(x); c = f(c); ctx = f(ctx); c_ctx = f(c_ctx)
    nc = bass.Bass("TRN2", target_bir_lowering=False)
    build(nc)
    cs = _consts()
    b_ada0 = f(b_ada)[0]
    lb = f(hgrn_lb_logits)
    shared = dict(
        w_ada=f(w_ada)[0], badaF=_pk(b_ada0),
        badaG=np.concatenate([_rep(b_ada0[2048:3072]), _rep(b_ada0[5120:6144])], axis=1),
        nwF=np.concatenate([_pk(f(norm_mix_w)[0]), _pk(f(norm_ffn_w)[0])], axis=1),
        w_in=f(w_in)[0],
        lbl=np.concatenate([_rep(lb[0, 0]), _rep(lb[0, 1]), _rep(lb[1, 0]), _rep(lb[1, 1])], axis=1),
        hnw=f(hgrn_norm_w)[0].reshape(128, 1).copy(),
        qkw=np.concatenate([_rep(np.tile(f(q_norm_w)[0], 8)), _rep(np.tile(f(k_norm_w)[0], 2))], axis=1),
        sinks=_rep(f(attn_sinks)[0]),
        w_bh=f(w_branch_hgrn)[0], w_ba=f(w_branch_attn)[0], w_o=f(w_out)[0],
        w_g=f(w_ffn_gate)[0], w_u=f(w_ffn_up)[0], w_d=f(w_ffn_down)[0],
        **cs)
    in_maps = []
    for b in range(8):
        m = dict(shared)
        m['x'] = x[b]
        m['ctx'] = ctx[b]
        m['c2'] = np.ascontiguousarray(np.stack([_pk(c[b]), _pk(c_ctx)], axis=2).reshape(128, 16))
        in_maps.append(m)
    res = run_bass_kernel_spmd(nc, in_maps, core_ids=list(range(8)))
    return np.stack([np.asarray(r["out"], np.float32) for r in res.results], axis=0)
```

```python
import numpy as np
import concourse.bass as bass
import concourse.mybir as mybir

F32 = mybir.dt.float32
BF16 = mybir.dt.bfloat16
AF = mybir.ActivationFunctionType
ALU = mybir.AluOpType
AX = mybir.AxisListType

ENGS = ['pe', 'act', 'dve', 'pool', 'sp']
ECODE = {e: i for i, e in enumerate(ENGS)}


def dsize(dt):
    return 4 if dt == F32 else 2


class Op:
    __slots__ = ('waits', 'fn', 'inc', 'dmakey')

    def __init__(self, fn):
        self.waits = []
        self.fn = fn
        self.inc = False
        self.dmakey = None


class Sched:
    BS = 64

    def __init__(self, nc, spaces):
        self.nc = nc
        self.ops = {e: [] for e in ENGS}
        self.seen = {e: {} for e in ENGS}
        self.seenD = {e: {} for e in ENGS}
        self.st = {}
        for name, (nbytes, bs) in spaces.items():
            n = (nbytes + bs - 1) // bs
            self.st[name] = dict(
                bs=bs,
                lw_eng=np.full(n, -1, np.int64), lw_idx=np.full(n, -1, np.int64),
                lw_dma=np.full(n, -1, np.int64),
                rd=np.full((len(ENGS), n), -1, np.int64),
                rd_dma={},
            )
        self.dmaevs = []
        self.dmacount = {}
        self.keys = {}

    def blocks(self, ap):
        name = ap.tensor.name
        st = self.st.get(name)
        if st is None:
            return None
        bs = st['bs']
        dsz = dsize(ap.dtype)
        dims = list(ap.ap)
        rowlen = ap.tensor.shape[-1]
        off = ap.offset % rowlen
        fd = list(dims[1:])
        inner = 1
        if fd and fd[-1][0] == 1:
            inner = fd[-1][1]
            fd = fd[:-1]
        starts = np.array([off], dtype=np.int64)
        for (s, c) in fd:
            if s == 0 or c == 1:
                continue
            starts = (starts[:, None] + (np.arange(c, dtype=np.int64) * s)[None, :]).ravel()
        b0 = (starts * dsz) // bs
        b1 = ((starts + inner) * dsz - 1) // bs
        if len(starts) == 1:
            idx = np.arange(b0[0], b1[0] + 1)
        else:
            span = int((b1 - b0).max()) + 1
            idx = (b0[:, None] + np.arange(span)[None, :])
            idx = idx[idx <= b1[:, None]]
            idx = np.unique(idx)
        return (name, idx)

    def _add(self, E, fn, reads, writes, rkeys=(), wkeys=(), dmakey=None):
        op = Op(fn)
        myidx = len(self.ops[E])
        ec = ECODE[E]
        deps_c = {}
        deps_d = set()
        R = [b for b in (self.blocks(a) for a in reads) if b is not None]
        W = [b for b in (self.blocks(a) for a in writes) if b is not None]

        def writers(st, idx):
            e = st['lw_eng'][idx]
            i = st['lw_idx'][idx]
            for code in np.unique(e):
                if code < 0:
                    continue
                m = int(i[e == code].max())
                if m > deps_c.get(int(code), -1):
                    deps_c[int(code)] = m
            d = st['lw_dma'][idx]
            for v in np.unique(d):
                if v >= 0:
                    deps_d.add(int(v))

        for (sp, idx) in R:
            writers(self.st[sp], idx)
            if sp == "psum":
                r = self.st[sp]['rd'][:, idx].max(axis=1)
                for code in range(len(ENGS)):
                    if r[code] >= 0 and code != ec:
                        if r[code] > deps_c.get(code, -1):
                            deps_c[code] = int(r[code])
        for (sp, idx) in W:
            st = self.st[sp]
            writers(st, idx)
            r = st['rd'][:, idx].max(axis=1)
            for code in range(len(ENGS)):
                if r[code] >= 0 and (code != ec or E != 'pe') and r[code] < (myidx if code == ec else 1 << 60):
                    if r[code] > deps_c.get(code, -1):
                        deps_c[code] = int(r[code])
            if st['rd_dma']:
                for b in idx:
                    l = st['rd_dma'].get(int(b))
                    if l:
                        deps_d.update(l)
        for k in rkeys:
            ks = self.keys.get(k)
            if ks and ks['lw'] is not None:
                deps_d.add(ks['lw'])
        for k in wkeys:
            ks = self.keys.get(k)
            if ks:
                if ks['lw'] is not None:
                    deps_d.add(ks['lw'])
                deps_d.update(ks['rd'])

        for code, idx in sorted(deps_c.items()):
            Wn = ENGS[code]
            if Wn == 'pe' and E == 'pe':
                continue
            if idx <= self.seen[E].get(Wn, -1):
                continue
            op.waits.append(('c', Wn, idx))
            self.seen[E][Wn] = idx
            self.ops[Wn][idx].inc = True
        for evid in sorted(deps_d):
            sk, val = self.dmaevs[evid]
            if val <= self.seenD[E].get(sk, 0):
                continue
            op.waits.append(('d', sk, val))
            self.seenD[E][sk] = val

        evid = None
        if dmakey is not None:
            self.dmacount[dmakey] = self.dmacount.get(dmakey, 0) + 1
            evid = len(self.dmaevs)
            self.dmaevs.append((dmakey, 16 * self.dmacount[dmakey]))
            op.dmakey = dmakey
        for (sp, idx) in R:
            st = self.st[sp]
            if evid is None:
                st['rd'][ec, idx] = myidx
            else:
                for b in idx:
                    st['rd_dma'].setdefault(int(b), []).append(evid)
        for (sp, idx) in W:
            st = self.st[sp]
            if evid is None:
                st['lw_eng'][idx] = ec
                st['lw_idx'][idx] = myidx
                st['lw_dma'][idx] = -1
            else:
                st['lw_eng'][idx] = -1
                st['lw_idx'][idx] = -1
                st['lw_dma'][idx] = evid
            st['rd'][:, idx] = -1
            if st['rd_dma']:
                for b in idx:
                    st['rd_dma'].pop(int(b), None)
        if evid is not None:
            for k in rkeys:
                self.keys.setdefault(k, dict(lw=None, rd=[]))['rd'].append(evid)
            for k in wkeys:
                self.keys[k] = dict(lw=evid, rd=[])
        self.ops[E].append(op)
        return op

    def mm(self, out, lhsT, rhs, start=True, stop=True, **kw):
        return self._add('pe', lambda e: e.matmul(out, lhsT, rhs, start=start, stop=stop, **kw),
                         [lhsT, rhs], [out])

    def tr(self, out, in_, ident):
        return self._add('pe', lambda e: e.transpose(out, in_, ident), [in_, ident], [out])

    def act(self, out, in_, func, bias=None, scale=None, accum_out=None):
        kw = {}
        reads = [in_]
        writes = [out]
        if bias is not None:
            kw['bias'] = bias
            if not isinstance(bias, (int, float)):
                reads.append(bias)
        if scale is not None:
            kw['scale'] = scale
            if not isinstance(scale, (int, float)):
                reads.append(scale)
        if accum_out is not None:
            kw['accum_out'] = accum_out
            writes.append(accum_out)
        return self._add('act', lambda e: e.activation(out=out, in_=in_, func=func, **kw), reads, writes)

    def tt(self, eng, out, in0, in1, op):
        return self._add(eng, lambda e: e.tensor_tensor(out=out, in0=in0, in1=in1, op=op), [in0, in1], [out])

    def ts(self, eng, out, in0, s1, s2, op0, op1=None, accum_out=None):
        reads = [in0]
        for s in (s1, s2):
            if s is not None and not isinstance(s, (int, float)):
                reads.append(s)
        writes = [out]
        kw = {}
        if op1 is not None:
            kw['op1'] = op1
        if accum_out is not None:
            kw['accum_out'] = accum_out
            writes.append(accum_out)
        return self._add(eng, lambda e: e.tensor_scalar(out=out, in0=in0, scalar1=s1, scalar2=s2, op0=op0, **kw),
                         reads, writes)

    def stt(self, eng, out, in0, scalar, in1, op0, op1):
        reads = [in0, in1]
        if not isinstance(scalar, (int, float)):
            reads.append(scalar)
        return self._add(eng, lambda e: e.scalar_tensor_tensor(out=out, in0=in0, scalar=scalar, in1=in1,
                                                              op0=op0, op1=op1), reads, [out])

    def copy(self, eng, out, in_):
        if eng == 'act':
            return self._add('act', lambda e: e.copy(out=out, in_=in_), [in_], [out])
        return self._add(eng, lambda e: e.tensor_copy(out=out, in_=in_), [in_], [out])

    def reduce(self, eng, out, in_, op, axis=AX.X):
        return self._add(eng, lambda e: e.tensor_reduce(out=out, in_=in_, axis=axis, op=op), [in_], [out])

    def recip(self, out, in_):
        return self._add('dve', lambda e: e.reciprocal(out=out, in_=in_), [in_], [out])

    def memset(self, eng, out, val):
        return self._add(eng, lambda e: e.memset(out, val), [], [out])

    def dma(self, q, out, in_, key, rkeys=(), wkeys=()):
        return self._add(q, lambda e: e.dma_start(out=out, in_=in_), [in_], [out],
                         rkeys=rkeys, wkeys=wkeys, dmakey=key)

    def finish(self, q='sp'):
        op = Op(None)
        for sk, cnt in self.dmacount.items():
            val = 16 * cnt
            if val > self.seenD[q].get(sk, 0):
                op.waits.append(('d', sk, val))
                self.seenD[q][sk] = val
        self.ops[q].append(op)

    def emit(self):
        nc = self.nc
        sems = {e: nc.alloc_semaphore("s_" + e) for e in ENGS}
        dsems = {k: nc.alloc_semaphore("d_" + k) for k in self.dmacount}
        rank = {}
        for e in ENGS:
            r = 0
            rk = []
            for op in self.ops[e]:
                if op.inc:
                    r += 1
                rk.append(r)
            rank[e] = rk

        def run(e, eng):
            for op in self.ops[e]:
                for w in op.waits:
                    if w[0] == 'c':
                        eng.wait_ge(sems[w[1]], rank[w[1]][w[2]])
                    else:
                        eng.wait_ge(dsems[w[1]], w[2])
                if op.fn is None:
                    continue
                ins = op.fn(eng)
                if op.dmakey is not None:
                    ins.then_inc(dsems[op.dmakey], 16)
                elif op.inc:
                    ins.then_inc(sems[e], 1)

        with nc.Block() as block:
            block.tensor(lambda t: run('pe', t))
            block.scalar(lambda t: run('act', t))
            block.vector(lambda t: run('dve', t))
            block.gpsimd(lambda t: run('pool', t))
            block.sync(lambda t: run('sp', t))
        return {e: len(self.ops[e]) for e in ENGS}


class Arena:
    def __init__(self, nc, name, nbytes):
        self.name = name
        self.nbytes = nbytes
        self.t32 = nc.alloc_sbuf_tensor(name, [128, nbytes // 4], F32)
        self.t16 = self.t32.bitcast(BF16)
        self.free = [(0, nbytes)]
        self.peak = 0

    def alloc(self, nbytes):
        nbytes = (nbytes + 63) // 64 * 64
        if nbytes >= 8192:
            for i in range(len(self.free) - 1, -1, -1):
                o, n = self.free[i]
                if n >= nbytes:
                    if n == nbytes:
                        self.free.pop(i)
                    else:
                        self.free[i] = (o, n - nbytes)
                    self.peak = max(self.peak, o + n)
                    return o + n - nbytes, nbytes
        else:
            for i, (o, n) in enumerate(self.free):
                if n >= nbytes:
                    if n == nbytes:
                        self.free.pop(i)
                    else:
                        self.free[i] = (o + nbytes, n - nbytes)
                    self.peak = max(self.peak, o + nbytes)
                    return o, nbytes
        raise RuntimeError("arena full: need %d, free=%s" % (nbytes, self.free))

    def release(self, off, nbytes):
        self.free.append((off, nbytes))
        self.free.sort()
        m = []
        for o, n in self.free:
            if m and m[-1][0] + m[-1][1] == o:
                m[-1] = (m[-1][0], m[-1][1] + n)
            else:
                m.append((o, n))
        self.free = m

    def buf(self, dtype, shape):
        return Buf(self, dtype, shape)


_L = "abcdefgh"


class Buf:
    def __init__(self, arena, dtype, shape):
        self.arena = arena
        self.dtype = dtype
        self.shape = list(shape)
        n = int(np.prod(shape[1:]))
        self.off, self.nbytes = arena.alloc(n * dsize(dtype))
        t = arena.t32 if dtype == F32 else arena.t16
        e0 = self.off // dsize(dtype)
        ap = t[0:shape[0], e0:e0 + n]
        if len(shape) > 2:
            names = [_L[i] for i in range(len(shape) - 1)]
            pat = "p (%s) -> p %s" % (" ".join(names), " ".join(names))
            ap = ap.rearrange(pat, **{names[i]: shape[i + 1] for i in range(len(names) - 1)})
        self.ap = ap

    def __getitem__(self, k):
        return self.ap[k]

    def free(self):
        self.arena.release(self.off, self.nbytes)


def mkap(ap, extra_off, dims):
    return bass.AP(ap.tensor, ap.offset + extra_off, [list(ap.ap[0])] + [list(d) for d in dims])

from concourse.bass_utils import run_bass_kernel_spmd
import ml_dtypes

P = 128
S_LEN = 2048
NT = 16
D = 1024
KC = 8
DFF = 2816
NFF = 22
EPS = 1e-6
CTX = 256
IN_COLS = 5376


def build(nc, stop=None):
    SB_BYTES = 206 * 1024
    A = Arena(nc, "arena", SB_BYTES)
    ps32 = nc.alloc_psum_tensor("psum", [128, 4096], F32)
    ps16 = ps32.bitcast(BF16)
    S = Sched(nc, {"arena": (SB_BYTES, 64), "psum": (16384, 2048)})

    def bank(b):
        return ps32[:, b * 512:(b + 1) * 512]

    def bank16(b):
        return ps16[:, b * 1024:(b + 1) * 1024]

    def din(name, shape, dt=F32):
        return nc.dram_tensor(name, list(shape), dt, kind="ExternalInput").ap()

    x = din("x", [S_LEN, D])
    ctx = din("ctx", [CTX, D])
    c2 = din("c2", [128, 16])
    w_ada = din("w_ada", [D, 6 * D])
    badaF = din("badaF", [128, 48])
    badaG = din("badaG", [128, 2048])
    nwF = din("nwF", [128, 16])
    w_in = din("w_in", [D, IN_COLS])
    lbl = din("lbl", [128, 2048])
    hnw_d = din("hnw", [128, 1])
    qkw = din("qkw", [128, 640])
    sinks = din("sinks", [128, 8])
    w_bh = din("w_bh", [512, D])
    w_ba = din("w_ba", [512, D])
    w_o = din("w_o", [D, D])
    w_g = din("w_g", [D, DFF])
    w_u = din("w_u", [D, DFF])
    w_d = din("w_d", [DFF, D])
    ident_d = din("ident", [128, 128], BF16)
    mats_d = din("mats", [128, 5 * 128])
    sel4_d = din("sel4", [128, 4])
    bmask_d = din("bmask", [128, 256], BF16)
    ropeC_d = din("ropeC", [128, 16 * 64])
    ropeS_d = din("ropeS", [128, 16 * 64])
    out = nc.dram_tensor("out", [S_LEN, D], F32, kind="ExternalOutput").ap()

    _n = [0]

    def ld(dst, src, q='sp'):
        _n[0] += 1
        S.dma(q, dst, src, "c%d" % _n[0])

    def wload(dst, src2d, key):
        if not key.startswith("f"):
            _n[0] += 1
            key = "w%d" % _n[0]
        S.dma('pool', dst, src2d.rearrange("(k p) n -> p k n", p=128), key)

    ident = A.buf(BF16, [128, 128]); ld(ident.ap, ident_d)
    mats = A.buf(F32, [128, 5, 128]); ld(mats.ap, mats_d.rearrange("p (m c) -> p m c", m=5))
    M_LE, M_GT, M_GE, M_LT, ONESD = [mats[:, i, :] for i in range(5)]
    sel4 = A.buf(F32, [128, 4]); ld(sel4.ap, sel4_d)
    matsb = A.buf(BF16, [128, 4, 128])
    S.copy('dve', matsb.ap, mats[:, 0:4, :])
    MB_LE, MB_GT, MB_GE, MB_LT = [matsb[:, i, :] for i in range(4)]
    sel4b = A.buf(BF16, [128, 4])
    S.copy('dve', sel4b.ap, sel4.ap)
    hnw = A.buf(F32, [128, 1]); ld(hnw.ap, hnw_d)
    G1B = A.buf(F32, [128, 1024]); G2B = A.buf(F32, [128, 1024])
    AB = A.buf(F32, [128, 6, 8])
    HT = A.buf(BF16, [128, 8, S_LEN])
    OB = A.buf(F32, [128, 4, S_LEN])
    Sst = {'f': A.buf(F32, [128, 512]), 'b': A.buf(F32, [128, 512])}
    NS = [[A.buf(F32, [128, 1]) for _ in range(3)] for _ in range(4)]
    NB_ = {'XS': [A.buf(F32, [128, 1024]) for _ in range(5)]}
    NB_['xsb'] = [A.buf(BF16, [128, 1024]) for _ in range(4)]
    NB_['ntmp'] = [A.buf(F32, [128, 8, 128]) for _ in range(2)]
    CKT = A.buf(BF16, [128, 2, CTX])
    CV = A.buf(BF16, [128, 2, 2, 128])
    kw_ = A.buf(F32, [128, 640]); ld(kw_.ap, qkw)

    c2s = A.buf(F32, [128, 8, 2]); ld(c2s.ap, c2.rearrange("p (k t) -> p k t", k=8))
    bF = A.buf(F32, [128, 48]); ld(bF.ap, badaF)
    nws = A.buf(F32, [128, 2, 8]); ld(nws.ap, nwF.rearrange("p (a k) -> p a k", a=2))
    ld(G1B.ap, badaG[:, 0:1024]); ld(G2B.ap, badaG[:, 1024:2048])
    sg0 = A.buf(F32, [128, 8, 2])
    sc2 = A.buf(F32, [128, 8, 2])
    S.act(sg0.ap, c2s.ap, AF.Sigmoid)
    S.tt('dve', sc2.ap, c2s.ap, sg0.ap, ALU.mult)
    SCB = A.buf(BF16, [128, 8, 128])
    S.copy('dve', SCB.ap, mkap(sc2.ap, 0, [[2, 8], [0, 128]]))
    sc2b = A.buf(BF16, [128, 8, 2])
    S.copy('dve', sc2b.ap, sc2.ap)
    MODF = A.buf(F32, [128, 4, 8, 2])
    WA = [A.buf(BF16, [128, 8, 512]) for _ in range(4)]
    wa_slot = lambda cb: cb if cb < 4 else cb % 2
    tmpA = A.buf(F32, [128, 8])
    fsec = {0: 0, 1: 0, 2: 1, 3: 1, 6: 2, 7: 2, 8: 3, 9: 3}
    fbias = {0: 0, 1: 8, 2: 24, 3: 32}
    gsec = {4: (G1B, 0), 5: (G1B, 1), 10: (G2B, 0), 11: (G2B, 1)}

    def ada_load(cb):
        S.dma('pool', WA[wa_slot(cb)].ap, w_ada[:, cb * 512:(cb + 1) * 512].rearrange("(k p) n -> p k n", p=128),
              "wa%d" % wa_slot(cb))

    def ada_block(cb, pb):
        wa = WA[wa_slot(cb)]
        if cb in fsec:
            sec = fsec[cb]
            for m in range(4):
                for k in range(8):
                    S.mm(bank(pb)[:, 2 * m:2 * m + 2], wa[:, k, m * 128:(m + 1) * 128], sc2b[:, k, :],
                         start=(k == 0), stop=(k == 7))
            c0 = (cb % 2) * 4
            S.tt('dve', MODF[:, sec, c0:c0 + 4, :], bank(pb)[:, 0:8].rearrange("p (c t) -> p c t", t=2),
                 mkap(bF.ap, fbias[sec] + c0, [[1, 4], [0, 2]]), ALU.add)
        else:
            gb, half = gsec[cb]
            for k in range(8):
                S.mm(bank(pb), SCB[:, k, :], wa[:, k, :], start=(k == 0), stop=(k == 7))
            S.tt('dve', gb[:, half * 512:(half + 1) * 512], gb[:, half * 512:(half + 1) * 512], bank(pb), ALU.add)

    def ada_AB(which):
        for (ai, nwi, scsec, t) in which[0]:
            S.ts('dve', tmpA.ap, MODF[:, scsec, :, t], 1.0, None, ALU.add)
            S.tt('dve', AB[:, ai, :], tmpA.ap, nws[:, nwi, :], ALU.mult)
        for (bi, shsec, t) in which[1]:
            S.copy('dve', AB[:, bi, :], MODF[:, shsec, :, t])

    ada_order = [0, 1, 2, 3, 4, 5, 6, 7, 8, 9, 10, 11]
    for i_ in range(4):
        ada_load(i_)
    for i_ in range(4):
        ada_block(i_, i_ % 2)
    ada_load(4); ada_load(5)
    WA[2].free(); WA[3].free()
    ada_AB((((0, 0, 1, 0), (2, 0, 1, 1)), ((1, 0, 0), (3, 0, 1))))
    ada_next = [4]

    def ada_step():
        cb = ada_next[0]
        if cb >= 12:
            return
        ada_block(cb, 4)
        if cb + 2 < 12:
            ada_load(cb + 2)
        ada_next[0] += 1
        if ada_next[0] == 12:
            ada_AB((((4, 1, 3, 0),), ((5, 2, 0),)))
            for b_ in [c2s, bF, nws, sg0, sc2, sc2b, SCB, MODF, tmpA] + WA[:2]:
                b_.free()

    def norm_gen(xt, ai, bi, dst_fn, par, pb=None):
        ss, t1s, rs = NS[par]
        xsb = NB_['xsb'][par]
        if pb is None:
            pb = 6 + par
        S.memset('dve', ss.ap, 0.0); yield
        S.act(xsb.ap, xt, AF.Square, accum_out=ss.ap); yield
        S.act(t1s.ap, ss.ap, AF.Ln, bias=EPS, scale=1.0 / D); yield
        S.act(rs.ap, t1s.ap, AF.Exp, scale=-0.5); yield
        S.tt('pool', xsb.ap, xt, mkap(rs.ap, 0, [[0, 1024]]), ALU.mult); yield
        for k in range(8):
            S.tr(bank16(pb)[:, k * 128:(k + 1) * 128], xsb[:, k * 128:(k + 1) * 128], ident.ap)
        yield
        for k in range(8):
            if par % 2 == 0:
                S.act(dst_fn(k), bank16(pb)[:, k * 128:(k + 1) * 128], AF.Identity,
                      scale=AB[:, ai, k:k + 1], bias=AB[:, bi, k:k + 1])
                yield
        if par % 2 == 1:
            tmp = NB_['ntmp'][par // 2]
            d3 = mkap(dst_fn(0), 0, [[dst_fn(1).offset - dst_fn(0).offset, 8], [1, 128]])
            S.tt('dve', tmp.ap, bank16(pb).rearrange("p (a b) -> p a b", a=8),
                 mkap(AB.ap, ai * 8, [[1, 8], [0, 128]]), ALU.mult); yield
            S.tt('dve', d3, tmp.ap, mkap(AB.ap, bi * 8, [[1, 8], [0, 128]]), ALU.add); yield

    def norm_to_HT(xt, ai, bi, dst_fn, pb):
        for _ in norm_gen(xt, ai, bi, dst_fn, pb - 6):
            pass

    def run_pipelined(gens, depth=2, stagger=1):
        active = []
        it = iter(gens)
        more = True
        rounds = 0
        while True:
            if more and len(active) < depth and rounds % stagger == 0:
                g = next(it, None)
                if g is None:
                    more = False
                else:
                    active.append(g)
            if not active and not more:
                break
            for g in list(active):
                if next(g, 'x') == 'x':
                    active.remove(g)
            rounds += 1

    def proj_T(b, lhs_fn, W, c0, n):
        for k in range(8):
            S.mm(bank(b)[:, 0:n], lhs_fn(k), W[:, k, c0:c0 + n], start=(k == 0), stop=(k == 7))

    OML = {'f': A.buf(F32, [128, 512]), 'b': A.buf(F32, [128, 512])}
    lbs = A.buf(F32, [128, 4, 512]); ld(lbs.ap, lbl.rearrange("p (a n) -> p a n", a=4))
    dlt = A.buf(F32, [128, 512])
    for di, d_ in enumerate(('f', 'b')):
        S.tt('dve', dlt.ap, lbs[:, 2 * di + 1, :], lbs[:, 2 * di, :], ALU.subtract)
        S.act(OML[d_].ap, dlt.ap, AF.Sigmoid)
    lbs.free(); dlt.free()
    hw = {}

    def alloc_hw():
        for nm in ('sgn', 'kk', 'lf', 'E3', 'E2', 'qs', 'osum'):
            hw[nm] = A.buf(F32, [128, 512])
        hw['E1'] = hw['sgn']
        hw['sq'] = hw['E3']
        for nm in ('qd', 'kd', 'lfh', 'lfl'):
            hw[nm] = A.buf(BF16, [128, 512])
        for nm in ('vb', 'kdec'):
            hw[nm] = [A.buf(BF16, [128, 512]) for _ in range(2)]
        hw['VM'] = [A.buf(BF16, [128, 4, 512]) for _ in range(2)]
        hw['DEC'] = [A.buf(F32, [128, 16]) for _ in range(2)]
        hw['QKT'] = [A.buf(BF16, [128, 8, 128]) for _ in range(2)]
        hw['scT'] = A.buf(BF16, [128, 4, 128])
        hw['SBs'] = [A.buf(BF16, [128, 4, 512]) for _ in range(2)]

    LNC = float(np.log(128.0 ** -0.5))

    def hgrn_proj(spec, which):
        d, lhs_fn, W, cf, ci, cq = spec[:6]
        if which == 0:
            proj_T(0, lhs_fn, W, cf, 512)
        elif which == 1:
            proj_T(1, lhs_fn, W, ci, 512)
        elif cq is not None:
            proj_T(2, lhs_fn, W, cq, 512)

    def hgrnA(spec, par, nxt):
        d, lhs_fn, W, cf, ci, cq = spec[:6]
        with_q = cq is not None
        Mcum, Mrev = (MB_LE, MB_GT) if d == 'f' else (MB_GE, MB_LT)
        vb, kdec, VM, DEC = hw['vb'][par], hw['kdec'][par], hw['VM'][par], hw['DEC'][par]
        S.act(hw['sgn'].ap, bank(0), AF.Sigmoid, scale=-1.0); yield
        if with_q:
            S.act(hw['sq'].ap, bank(2), AF.Sigmoid); yield
        S.copy('act', vb.ap, bank(1)); yield
        if nxt is not None:
            hgrn_proj(nxt, 0); yield
        S.tt('dve', hw['kk'].ap, hw['sgn'].ap, OML[d].ap, ALU.mult); yield
        if with_q:
            S.tt('dve', hw['qs'].ap, bank(2), hw['sq'].ap, ALU.mult); yield
        S.act(hw['lf'].ap, hw['kk'].ap, AF.Ln, bias=1.0, scale=-1.0); yield
        for j in range(4):
            S.tt('pool', VM[:, j, :], vb.ap, mkap(sel4.ap, j, [[0, 512]]), ALU.mult); yield
        S.copy('dve', hw['lfh'].ap, hw['lf'].ap); yield
        S.tt('dve', hw['lfl'].ap, hw['lf'].ap, hw['lfh'].ap, ALU.subtract); yield
        S.mm(bank(3), Mrev, hw['lfh'].ap, start=True, stop=False)
        S.mm(bank(3), Mrev, hw['lfl'].ap, start=False, stop=True)
        yield
        for h in range(4):
            S.mm(bank(7)[:, h * 4:(h + 1) * 4], hw['lfh'][:, h * 128:(h + 1) * 128], sel4b.ap, start=True, stop=False)
            S.mm(bank(7)[:, h * 4:(h + 1) * 4], hw['lfl'][:, h * 128:(h + 1) * 128], sel4b.ap, start=False, stop=True)
        S.act(DEC.ap, bank(7)[:, 0:16], AF.Exp); yield
        S.act(hw['E3'].ap, bank(3), AF.Exp); yield
        if nxt is not None:
            hgrn_proj(nxt, 1); yield
        S.tt('dve', kdec.ap, hw['kk'].ap, hw['E3'].ap, ALU.mult); yield
        if with_q:
            S.mm(bank(3), Mcum, hw['lfh'].ap, start=True, stop=False)
            S.mm(bank(3), Mcum, hw['lfl'].ap, start=False, stop=True); yield
            S.act(hw['E1'].ap, bank(3), AF.Exp, bias=LNC); yield
            S.act(hw['E2'].ap, bank(3), AF.Exp, scale=-1.0); yield
            if nxt is not None:
                hgrn_proj(nxt, 2); yield
            S.tt('dve', hw['qd'].ap, hw['qs'].ap, hw['E1'].ap, ALU.mult); yield
            S.tt('dve', hw['kd'].ap, hw['kk'].ap, hw['E2'].ap, ALU.mult); yield
            for h in range(4):
                S.tr(bank16(6)[:, h * 128:(h + 1) * 128], hw['qd'][:, h * 128:(h + 1) * 128], ident.ap)
            yield
            for h in range(4):
                S.tr(bank16(6)[:, (4 + h) * 128:(5 + h) * 128], hw['kd'][:, h * 128:(h + 1) * 128], ident.ap)
            yield
            S.copy('act', hw['QKT'][par].ap, bank16(6).rearrange("p (a b) -> p a b", a=8)); yield
        elif nxt is not None:
            hgrn_proj(nxt, 2); yield

    def hgrnB(d, with_q, par, out_cb):
        St = Sst[d]
        msk = M_LE if d == 'f' else M_GE
        vb, kdec, VM, DEC = hw['vb'][par], hw['kdec'][par], hw['VM'][par], hw['DEC'][par]
        QKT = hw['QKT'][par]
        SBs = hw['SBs'][par]
        if with_q:
            for h in range(4):
                S.mm(bank(7)[:, h * 128:(h + 1) * 128], QKT[:, 4 + h, :], QKT[:, h, :])
            S.tt('dve', hw['scT'].ap, bank(7).rearrange("p (a b) -> p a b", a=4),
                 mkap(msk, 0, [[0, 4], [1, 128]]), ALU.mult); yield
        order = (0, 1, 2, 3) if d == 'f' else (3, 2, 1, 0)
        for si, j in enumerate(order):
            ub = bank(4 + (si % 2))
            for h in range(4):
                S.mm(ub[:, h * 128:(h + 1) * 128], kdec[:, h * 128:(h + 1) * 128],
                     VM[:, j, h * 128:(h + 1) * 128])
            yield
            for h in range(4):
                sl = slice(h * 128, (h + 1) * 128)
                S.stt('dve', St[:, sl], St[:, sl], DEC[:, h * 4 + j:h * 4 + j + 1], ub[:, sl], ALU.mult, ALU.add)
                yield
            if si < 3:
                dst = SBs[:, order[si + 1], :]
            else:
                dst = hw['SBs'][1 - par][:, order[0], :]
            S.copy('act', dst, St.ap); yield
        if with_q:
            for h in range(4):
                S.mm(bank(5)[:, h * 128:(h + 1) * 128], vb[:, h * 128:(h + 1) * 128], hw['scT'][:, h, :],
                     start=True, stop=False, skip_group_check=True)
                for j in range(4):
                    S.mm(bank(5)[:, h * 128 + 32 * j:h * 128 + 32 * j + 32],
                         SBs[:, j, h * 128:(h + 1) * 128], QKT[:, h, 32 * j:32 * j + 32],
                         start=False, stop=(j == 3), skip_group_check=True)
                yield
            if out_cb is not None:
                for _ in out_cb(bank(5)):
                    yield

    class Pipe:
        def __init__(self):
            self.n = 0

        def run(self, specs, between=None):
            if not specs:
                return
            for w_ in range(3):
                hgrn_proj(specs[0], w_)
            prevB = None
            prevC = None
            n = len(specs)
            for i in range(n + 2):
                gens = []
                if prevC is not None:
                    gens.append((prevC(bank(5)), 8.0))
                prevC = None
                if prevB is not None:
                    gens.append((hgrnB(*prevB[0]), 30.0 if prevB[0][1] else 25.0))
                    prevC = prevB[1]
                prevB = None
                if i < n:
                    spec = specs[i]
                    par = self.n % 2
                    self.n += 1
                    nxt = specs[i + 1] if i + 1 < n else None
                    gens.append((hgrnA(spec, par, nxt), 27.0 if spec[5] is not None else 17.0))
                    prevB = ((spec[0], spec[5] is not None, par, spec[6]), spec[7])
                live = [[g_, 0, tot_] for (g_, tot_) in gens]
                while live:
                    e_ = min(live, key=lambda r: (r[1] + 1.0) / r[2])
                    if next(e_[0], 'x') == 'x':
                        live.remove(e_)
                    else:
                        e_[1] += 1
                if between is not None and i < n:
                    between(i)

        def init_snapshot(self, d):
            first = 0 if d == 'f' else 3
            S.copy('act', hw['SBs'][self.n % 2][:, first, :], Sst[d].ap)

    pipe = Pipe()

    WHG = A.buf(BF16, [128, 8, 2048])
    wload(WHG[:, :, 0:512], w_in[:, 0:512], "w")
    wload(WHG[:, :, 1536:2048], w_in[:, 512:1024], "w")
    wload(WHG[:, :, 512:1024], w_in[:, 1024:1536], "w")
    Wkv = A.buf(BF16, [128, 8, 256]); wload(Wkv.ap, w_in[:, 1536:1792], "w")
    wload(WHG[:, :, 1024:1536], w_in[:, 1792:2304], "w")
    HC = A.buf(BF16, [128, 8, CTX])
    p2tiles = list(range(NT - 1, -1, -1))

    def p2_load(n_):
        if n_ < NT:
            t_ = p2tiles[n_]
            S.dma('sp', NB_['XS'][n_ % 5].ap, x[t_ * 128:(t_ + 1) * 128, :], "xs%d" % (n_ % 5))

    def p2a_gen(n_):
        t_ = p2tiles[n_]
        for _ in norm_gen(NB_['XS'][n_ % 5].ap, 0, 1, lambda k: HT[:, k, t_ * 128:(t_ + 1) * 128], n_ % 4, pb=4 + n_ % 4):
            yield
        p2_load(n_ + 5)

    for n_ in range(5):
        p2_load(n_)
    run_pipelined((p2a_gen(n_) for n_ in range(NT)), depth=4, stagger=5)
    for t in range(2):
        xs = NB_['XS'][t % 2]
        S.dma('sp', xs.ap, ctx[t * 128:(t + 1) * 128, :], "xs%d" % (t % 2))
        norm_to_HT(xs.ap, 2, 3, lambda k, t=t: HC[:, k, t * 128:(t + 1) * 128], 6)
    for b_ in NB_['XS'] + NB_['xsb'] + NB_['ntmp']:
        b_.free()
    alloc_hw()
    S.memset('dve', Sst['f'].ap, 0.0)
    S.memset('dve', Sst['b'].ap, 0.0)
    S.memset('dve', CV.ap, 1.0)
    ckr = A.buf(F32, [128, 128]); cks = A.buf(F32, [128, 128]); css = A.buf(F32, [128, 2])
    ct1 = A.buf(F32, [128, 2]); crs = A.buf(F32, [128, 2]); ckb = A.buf(BF16, [128, 128])
    hc_fn = lambda t: (lambda k: HC[:, k, t * 128:(t + 1) * 128])
    pipe.run([('f', hc_fn(0), WHG, 0, 512, None, None, None), ('b', hc_fn(1), WHG, 1536, 512, None, None, None),
              ('f', hc_fn(1), WHG, 0, 512, None, None, None), ('b', hc_fn(0), WHG, 1536, 512, None, None, None)])
    for t in range(2):
        lf_ = hc_fn(t)
        proj_T(5, lf_, Wkv, 0, 256)
        S.copy('act', ckr.ap, bank(5)[:, 0:128])
        S.copy('dve', CV[:, t, :, 0:64], bank(5)[:, 128:256].rearrange("p (a b) -> p a b", a=2))
        S.tt('dve', cks.ap, ckr.ap, ckr.ap, ALU.mult)
        S.reduce('dve', css.ap, cks.ap.rearrange("p (a b) -> p a b", a=2), ALU.add)
        S.act(ct1.ap, css.ap, AF.Ln, bias=EPS, scale=1.0 / 64)
        S.act(crs.ap, ct1.ap, AF.Exp, scale=-0.5)
        S.tt('dve', cks.ap.rearrange("p (a b) -> p a b", a=2), ckr.ap.rearrange("p (a b) -> p a b", a=2),
             mkap(crs.ap, 0, [[1, 2], [0, 64]]), ALU.mult)
        S.tt('dve', ckb.ap, cks.ap, kw_[:, 512:640], ALU.mult)
        for kv in range(2):
            S.tr(bank16(6)[0:64, kv * 128:(kv + 1) * 128], ckb[:, kv * 64:(kv + 1) * 64], ident.ap)
        S.copy('act', CKT[0:64, :, t * 128:(t + 1) * 128],
               bank16(6)[0:64, 0:256].rearrange("p (a b) -> p a b", a=2))
    for b_ in (ckr, cks, css, ct1, crs, ckb, HC, Wkv):
        b_.free()

    if stop == 1:
        S.finish('sp'); return S.emit(), A.peak

    def ob_store(t):
        def cb(bk):
            S.copy('dve', OB[:, :, t * 128:(t + 1) * 128], bk.rearrange("p (a b) -> p a b", a=4)); yield
        return cb

    ht_fn = lambda t: (lambda k: HT[:, k, t * 128:(t + 1) * 128])
    pipe.init_snapshot('b')
    pipe.run([('b', ht_fn(t), WHG, 1536, 512, 1024, ob_store(t), None) for t in p2tiles],
             between=lambda i: ada_step() if i % 2 == 1 else None)
    while ada_next[0] < 12:
        ada_step()

    if stop == 2:
        S.finish('sp'); return S.emit(), A.peak
    YH = A.buf(BF16, [128, 4, S_LEN])
    GS = [A.buf(BF16, [128, 2, S_LEN]) for _ in range(2)]
    wload(WHG[:, :, 1536:2048], w_in[:, 2304:2816], "w")
    sgg = A.buf(F32, [128, 512]); osq = A.buf(F32, [128, 512]); rt1 = A.buf(F32, [128, 512])
    rstd = rt1; y1 = osq

    def readout(t):
        def cb(bk):
            tg = t % 4
            S.tt('dve', hw['osum'].ap.rearrange("p (a b) -> p a b", a=4), bk.rearrange("p (a b) -> p a b", a=4),
                 OB[:, :, t * 128:(t + 1) * 128], ALU.add); yield
            S.tt('pool', osq.ap, hw['osum'].ap, hw['osum'].ap, ALU.mult); yield
            S.mm(bank(7), ONESD, osq.ap)
            S.act(rt1.ap, bank(7), AF.Ln, bias=EPS); yield
            S.act(rstd.ap, rt1.ap, AF.Exp, scale=-0.5); yield
            S.tt('dve', y1.ap, hw['osum'].ap, rstd.ap, ALU.mult); yield
            for hh in range(2):
                S.tt('pool', y1[:, hh * 256:(hh + 1) * 256].rearrange("p (a b) -> p a b", a=2),
                     y1[:, hh * 256:(hh + 1) * 256].rearrange("p (a b) -> p a b", a=2),
                     GS[hh][:, :, t * 128:(t + 1) * 128], ALU.mult)
            yield
            S.ts('dve', YH[:, :, t * 128:(t + 1) * 128], y1.ap.rearrange("p (a b) -> p a b", a=4),
                 hnw[:, 0:1], None, ALU.mult); yield
        return cb

    for g in range(4):
        g0 = g * 512
        for m in range(4):
            for k in range(8):
                S.mm(bank(4 + m), WHG[:, k, 1536 + m * 128:1536 + (m + 1) * 128], HT[:, k, g0:g0 + 512],
                     start=(k == 0), stop=(k == 7))
            S.act(sgg.ap, bank(4 + m), AF.Sigmoid)
            S.tt('dve', GS[m // 2][:, m % 2, g0:g0 + 512], bank(4 + m), sgg.ap, ALU.mult)
    pipe.init_snapshot('f')
    pipe.run([('f', ht_fn(t), WHG, 0, 512, 1024, None, readout(t)) for t in range(NT)])
    for b_ in [WHG, sgg, osq, rt1, OB] + GS:
        b_.free()
    for nm, b_ in hw.items():
        if nm in ('E1', 'sq'):
            continue
        for bb in (b_ if isinstance(b_, list) else [b_]):
            bb.free()
    OML['f'].free(); OML['b'].free()

    if stop == 3:
        S.finish('sp'); return S.emit(), A.peak
    W4 = A.buf(BF16, [128, 8, 768])
    wload(W4[:, :, 0:512], w_in[:, 2816:3328], "w0")
    wload(W4[:, :, 512:768], w_in[:, 1536:1792], "w1")
    ropeC = A.buf(F32, [128, 16, 64]); ld(ropeC.ap, ropeC_d.rearrange("p (a b) -> p a b", a=16))
    ropeS = A.buf(F32, [128, 16, 64]); ld(ropeS.ap, ropeS_d.rearrange("p (a b) -> p a b", a=16))
    bmask = A.buf(BF16, [128, 2, 128]); ld(bmask.ap, bmask_d.rearrange("p (a b) -> p a b", a=2))
    esk = A.buf(F32, [128, 8]); ld(esk.ap, sinks)
    S.act(esk.ap, esk.ap, AF.Exp)
    YA = A.buf(BF16, [128, 4, S_LEN])
    KT = A.buf(BF16, [128, 2, S_LEN])
    VA = A.buf(BF16, [128, NT, 2, 128])
    S.memset('dve', VA.ap, 1.0)
    QT = [A.buf(BF16, [128, 8, 128]) for _ in range(3)]
    qkr = A.buf(F32, [128, 640]); qsq = A.buf(F32, [128, 640]); qss = A.buf(F32, [128, 10])
    qt1 = A.buf(F32, [128, 10]); qrs = A.buf(F32, [128, 10]); qn = A.buf(F32, [128, 640])
    r1 = A.buf(F32, [128, 640]); r2 = A.buf(F32, [128, 640]); qr = A.buf(BF16, [128, 640])
    PT = [A.buf(BF16, [128, 512]) for _ in range(10)]
    den = [A.buf(F32, [128, 4]) for _ in range(2)]; rden = [A.buf(F32, [128, 4]) for _ in range(2)]
    YT = [A.buf(BF16, [128, 512]) for _ in range(2)]

    def attn_proj(t):
        lf_ = lambda k: HT[:, k, t * 128:(t + 1) * 128]
        proj_T(0, lf_, W4, 0, 512)
        proj_T(1, lf_, W4, 512, 256)

    def attn_prep(t):
        S.act(qsq[:, 0:512], bank(0), AF.Square); yield
        S.act(qsq[:, 512:640], bank(1)[:, 0:128], AF.Square); yield
        S.copy('act', VA[:, t, :, 0:64], bank(1)[:, 128:256].rearrange("p (a b) -> p a b", a=2)); yield
        S.reduce('dve', qss.ap, qsq.ap.rearrange("p (a b) -> p a b", a=10), ALU.add); yield
        S.act(qt1.ap, qss.ap, AF.Ln, bias=EPS, scale=1.0 / 64); yield
        S.act(qrs.ap, qt1.ap, AF.Exp, scale=-0.5); yield
        S.tt('dve', qn[:, 0:512].rearrange("p (a b) -> p a b", a=8), bank(0).rearrange("p (a b) -> p a b", a=8),
             mkap(qrs.ap, 0, [[1, 8], [0, 64]]), ALU.mult); yield
        S.tt('dve', qn[:, 512:640].rearrange("p (a b) -> p a b", a=2), bank(1)[:, 0:128].rearrange("p (a b) -> p a b", a=2),
             mkap(qrs.ap, 8, [[1, 2], [0, 64]]), ALU.mult); yield
        if t + 1 < NT:
            attn_proj(t + 1); yield
        S.tt('dve', qn.ap, qn.ap, kw_.ap, ALU.mult); yield
        S.tt('dve', r1.ap.rearrange("p (a b) -> p a b", a=10), qn.ap.rearrange("p (a b) -> p a b", a=10),
             mkap(ropeC.ap, t * 64, [[0, 10], [1, 64]]), ALU.mult); yield
        for hf in range(2):
            S.tt('dve', mkap(r2.ap, 16 * hf, [[64, 10], [32, 2], [1, 16]]),
                 mkap(qn.ap, 16 * (1 - hf), [[64, 10], [32, 2], [1, 16]]),
                 mkap(ropeS.ap, t * 64 + 16 * hf, [[0, 10], [32, 2], [1, 16]]), ALU.mult); yield
        S.tt('dve', qr.ap, r1.ap, r2.ap, ALU.add); yield
        for h in range(8):
            S.tr(bank16(2)[0:64, h * 128:(h + 1) * 128], qr[:, h * 64:(h + 1) * 64], ident.ap)
        yield
        for kv in range(2):
            S.tr(bank16(3)[0:64, kv * 128:(kv + 1) * 128], qr[:, 512 + kv * 64:512 + (kv + 1) * 64], ident.ap)
        yield
        S.copy('act', QT[t % 3][0:64, :, :], bank16(2)[0:64, :].rearrange("p (a b) -> p a b", a=8)); yield
        S.copy('dve', KT[0:64, :, t * 128:(t + 1) * 128], bank16(3)[0:64, 0:256].rearrange("p (a b) -> p a b", a=2)); yield

    def attn_kv(i, kv):
        kts = [('c', 0), ('c', 1)]
        if i > 0:
            kts.append(('p', i - 1))
        kts.append(('s', i))
        if i < NT - 1:
            kts.append(('n', i + 1))
        q_ = QT[i % 3]
        PTk = PT[kv * 5:(kv + 1) * 5]
        bk = bank(4 + kv)
        oa = bank(6 + kv)
        for n, (kind, ti) in enumerate(kts):
            if kind == 'c':
                lhsT = CKT[0:64, kv, ti * 128:(ti + 1) * 128]
            else:
                lhsT = KT[0:64, kv, ti * 128:(ti + 1) * 128]
            S.mm(bk, lhsT, q_[0:64, kv * 4:(kv + 1) * 4, :]); yield
            S.act(PTk[n].ap, bk, AF.Exp, scale=0.125); yield
            if kind in ('p', 'n'):
                mi = 0 if kind == 'p' else 1
                S.tt('dve', PTk[n].ap.rearrange("p (a b) -> p a b", a=4), PTk[n].ap.rearrange("p (a b) -> p a b", a=4),
                     mkap(bmask.ap, mi * 128, [[0, 4], [1, 128]]), ALU.mult); yield
        for h in range(4):
            for n, (kind, ti) in enumerate(kts):
                vsrc = CV[:, ti, kv, 0:65] if kind == 'c' else VA[:, ti, kv, 0:65]
                S.mm(oa[:, h * 128:h * 128 + 65], PTk[n][:, h * 128:(h + 1) * 128], vsrc,
                     start=(n == 0), stop=(n == len(kts) - 1))
            yield
        oav = oa.rearrange("p (a b) -> p a b", a=4)
        S.tt('dve', den[kv].ap, oav[:, :, 64], esk[:, kv * 4:(kv + 1) * 4], ALU.add); yield
        S.recip(rden[kv].ap, den[kv].ap); yield
        S.tt('dve', YT[i % 2][:, kv * 256:(kv + 1) * 256].rearrange("p (a b) -> p a b", a=4), oav[:, :, 0:64],
             mkap(rden[kv].ap, 0, [[1, 4], [0, 64]]), ALU.mult); yield

    def attn_fin(i):
        for c in range(4):
            S.tr(bank16(4)[:, c * 128:(c + 1) * 128], YT[i % 2][:, c * 128:(c + 1) * 128], ident.ap)
        S.copy('act', YA[:, :, i * 128:(i + 1) * 128], bank16(4)[:, 0:512].rearrange("p (a b) -> p a b", a=4))

    def prop_run(items):
        live = [[g_, 0, n_] for (g_, n_) in items]
        while live:
            e_ = min(live, key=lambda r: (r[1] + 1.0) / r[2])
            if next(e_[0], 'x') == 'x':
                live.remove(e_)
            else:
                e_[1] += 1

    attn_proj(0)
    for t in range(NT + 2):
        items = []
        if t >= 2:
            items.append((attn_kv(t - 2, 0), 22.0))
            items.append((attn_kv(t - 2, 1), 22.0))
        if t < NT:
            items.append((attn_prep(t), 19.0))
        prop_run(items)
        if t >= 2:
            attn_fin(t - 2)
    for b_ in [W4, ropeC, ropeS, bmask, esk, KT, VA, qkr, qsq, qss, qt1, qrs, qn, r1, r2, qr, kw_] + QT + PT + den + rden + YT:
        b_.free()

    if stop == 4:
        S.finish('sp'); return S.emit(), A.peak
    Wg = A.buf(BF16, [128, 8, 2048])
    for i_ in range(4):
        wload(Wg[:, :, i_ * 512:(i_ + 1) * 512], w_in[:, 3328 + i_ * 512:3328 + (i_ + 1) * 512], "w%d" % i_)
    Wbh = A.buf(BF16, [128, 4, 1024]); Wba = A.buf(BF16, [128, 4, 1024])
    wload(Wbh.ap, w_bh, "w")
    wload(Wba.ap, w_ba, "w")
    MT = A.buf(BF16, [128, 8, S_LEN])
    sgh = A.buf(F32, [128, 512]); sga = A.buf(F32, [128, 512]); m1 = A.buf(F32, [128, 512]); m2 = A.buf(F32, [128, 512])
    it = 0
    for g in range(4):
        g0 = g * 512
        for c in range(8):
            pb = 4 * (it % 2)
            it += 1
            for k in range(8):
                S.mm(bank(pb), Wg[:, k, c * 128:(c + 1) * 128], HT[:, k, g0:g0 + 512], start=(k == 0), stop=(k == 7))
            for k in range(8):
                S.mm(bank(pb + 1), Wg[:, k, 1024 + c * 128:1024 + (c + 1) * 128], HT[:, k, g0:g0 + 512],
                     start=(k == 0), stop=(k == 7))
            for k in range(4):
                S.mm(bank(pb + 2), Wbh[:, k, c * 128:(c + 1) * 128], YH[:, k, g0:g0 + 512], start=(k == 0), stop=(k == 3))
            for k in range(4):
                S.mm(bank(pb + 3), Wba[:, k, c * 128:(c + 1) * 128], YA[:, k, g0:g0 + 512], start=(k == 0), stop=(k == 3))
            S.act(sgh.ap, bank(pb), AF.Sigmoid)
            S.act(sga.ap, bank(pb + 1), AF.Sigmoid)
            S.tt('dve', m1.ap, sgh.ap, bank(pb + 2), ALU.mult)
            S.tt('dve', m2.ap, sga.ap, bank(pb + 3), ALU.mult)
            S.tt('dve', MT[:, c, g0:g0 + 512], m1.ap, m2.ap, ALU.add)
    for b_ in (Wg, Wbh, Wba, sgh, sga, m1, m2, YH, YA):
        b_.free()

    if stop == 5:
        S.finish('sp'); return S.emit(), A.peak
    Wo = A.buf(BF16, [128, 8, 1024])
    WG = [A.buf(BF16, [128, 8, 256]) for _ in range(2)]
    WU = [A.buf(BF16, [128, 8, 256]) for _ in range(2)]

    def ffn_wload(blk):
        wload(WG[blk % 2].ap, w_g[:, blk * 256:(blk + 1) * 256], "fg%d" % (blk % 2))
        wload(WU[blk % 2].ap, w_u[:, blk * 256:(blk + 1) * 256], "fu%d" % (blk % 2))

    P5D = 4
    NB_['XS'] = [A.buf(F32, [128, 1024]) for _ in range(P5D + 1)]
    NB_['xsb'] = [A.buf(BF16, [128, 1024]) for _ in range(P5D)]
    NB_['ntmp'] = [A.buf(F32, [128, 8, 128]) for _ in range(2)]
    wload(Wo[:, :, 0:512], w_o[:, 0:512], "w0")
    wload(Wo[:, :, 512:1024], w_o[:, 512:1024], "w1")
    XN = [A.buf(F32, [128, 1024]) for _ in range(P5D)]
    ffn_wload(0); ffn_wload(1)

    def p5_load(t_):
        if t_ < NT:
            sl = t_ % (P5D + 1)
            S.dma('sp', NB_['XS'][sl].ap, x[t_ * 128:(t_ + 1) * 128, :], "xs%d" % sl)

    def p5_gen(t):
        sl = t % P5D
        xs = NB_['XS'][t % (P5D + 1)]
        xn = XN[sl]
        for hf in range(2):
            bk = bank(2 * sl + hf)
            for k in range(8):
                S.mm(bk, MT[:, k, t * 128:(t + 1) * 128], Wo[:, k, hf * 512:(hf + 1) * 512],
                     start=(k == 0), stop=(k == 7))
            yield
            S.tt('dve', xn[:, hf * 512:(hf + 1) * 512], bk, G1B[:, hf * 512:(hf + 1) * 512], ALU.mult); yield
        S.tt('dve', xn.ap, xn.ap, xs.ap, ALU.add); yield
        p5_load(t + P5D + 1)
        S.dma('sp', out[t * 128:(t + 1) * 128, :], xn.ap, "xn%d" % sl, wkeys=["o%d" % t]); yield
        for _ in norm_gen(xn.ap, 4, 5, lambda k: HT[:, k, t * 128:(t + 1) * 128], sl, pb=2 * sl):
            yield

    for t_ in range(P5D + 1):
        p5_load(t_)
    run_pipelined((p5_gen(t) for t in range(NT)), depth=P5D, stagger=5)
    for b_ in [Wo, MT] + NB_['xsb'] + NB_['ntmp'] + XN + NB_['XS'][3:]:
        b_.free()

    if stop == 6:
        S.finish('sp'); return S.emit(), A.peak
    Wd = A.buf(BF16, [128, NFF, 1024])
    HID = A.buf(BF16, [128, NFF, 1024])
    sgf = [A.buf(F32, [128, 512]) for _ in range(2)]
    fin = [A.buf(F32, [128, 1024]) for _ in range(2)]
    it = 0
    for half in range(2):
        t0 = half * 1024
        for blk in range(11):
            wg_ = WG[blk % 2]; wu_ = WU[blk % 2]
            if not (half == 0 and blk < 2):
                ffn_wload(blk)
            if half == 0:
                wload(Wd[:, 2 * blk:2 * blk + 2, :], w_d[2 * blk * 128:(2 * blk + 2) * 128, :], "w")
            for jj in range(2):
                j = blk * 2 + jj
                for g in range(2):
                    g0 = t0 + g * 512
                    pb = 2 * (it % 3)
                    sg_ = sgf[it % 2]
                    it += 1
                    for k in range(8):
                        S.mm(bank(pb), wg_[:, k, jj * 128:(jj + 1) * 128], HT[:, k, g0:g0 + 512],
                             start=(k == 0), stop=(k == 7))
                    for k in range(8):
                        S.mm(bank(pb + 1), wu_[:, k, jj * 128:(jj + 1) * 128], HT[:, k, g0:g0 + 512],
                             start=(k == 0), stop=(k == 7))
                    S.act(sg_.ap, bank(pb), AF.Silu)
                    S.tt('dve', HID[:, j, g * 512:(g + 1) * 512], sg_.ap, bank(pb + 1), ALU.mult)
        def p6_load(t_):
            if t_ < (half + 1) * 8:
                S.dma('sp', NB_['XS'][t_ % 3].ap, out[t_ * 128:(t_ + 1) * 128, :], "xs%d" % (t_ % 3), rkeys=["o%d" % t_])

        p6_load(half * 8); p6_load(half * 8 + 1)
        for tt_ in range(8):
            t = half * 8 + tt_
            xs = NB_['XS'][t % 3]
            fo = fin[t % 2]
            p6_load(t + 2)
            for hf in range(2):
                for j in range(NFF):
                    S.mm(bank(6 + hf), HID[:, j, tt_ * 128:(tt_ + 1) * 128], Wd[:, j, hf * 512:(hf + 1) * 512],
                         start=(j == 0), stop=(j == NFF - 1))
                S.tt('dve', fo[:, hf * 512:(hf + 1) * 512], bank(6 + hf), G2B[:, hf * 512:(hf + 1) * 512], ALU.mult)
            S.tt('pool', fo.ap, fo.ap, xs.ap, ALU.add)
            S.dma('sp', out[t * 128:(t + 1) * 128, :], fo.ap, "fo%d" % (t % 2), wkeys=["o%d" % t])
    S.finish('sp')
    counts = S.emit()
    return counts, A.peak


_CONST_CACHE = {}


def _consts():
    if _CONST_CACHE:
        return _CONST_CACHE
    p = np.arange(128)
    same = (p[:, None] // 32) == (p[None, :] // 32)
    LE = (same & (p[:, None] <= p[None, :])).astype(np.float32)
    GT = (same & (p[:, None] > p[None, :])).astype(np.float32)
    GE = (same & (p[:, None] >= p[None, :])).astype(np.float32)
    LT = (same & (p[:, None] < p[None, :])).astype(np.float32)
    ONESD = np.full((128, 128), 1.0 / 128.0, np.float32)
    mats = np.concatenate([LE, GT, GE, LT, ONESD], axis=1)
    sel4 = (p[:, None] // 32 == np.arange(4)[None, :]).astype(np.float32)
    maskP = (p[None, :] <= p[:, None]).astype(np.float32)
    maskN = (p[:, None] <= p[None, :]).astype(np.float32)
    bmask = np.concatenate([maskP, maskN], axis=1).astype(ml_dtypes.bfloat16)
    t = np.arange(S_LEN)
    rows = (t // 64).astype(np.float32)
    cols = (t % 64).astype(np.float32)
    inv = (10000.0 ** (-np.arange(0, 32, 2, dtype=np.float32) / 32)).astype(np.float32)
    angR = rows[:, None] * inv[None, :]
    angC = cols[:, None] * inv[None, :]
    C = np.concatenate([np.cos(angR), np.cos(angR), np.cos(angC), np.cos(angC)], axis=1)
    Sn = np.concatenate([-np.sin(angR), np.sin(angR), -np.sin(angC), np.sin(angC)], axis=1)
    ropeC = C.reshape(16, 128, 64).transpose(1, 0, 2).reshape(128, 1024).astype(np.float32)
    ropeS = Sn.reshape(16, 128, 64).transpose(1, 0, 2).reshape(128, 1024).astype(np.float32)
    _CONST_CACHE.update(dict(
        ident=np.eye(128).astype(ml_dtypes.bfloat16), mats=np.ascontiguousarray(mats), sel4=sel4,
        bmask=np.ascontiguousarray(bmask), ropeC=np.ascontiguousarray(ropeC), ropeS=np.ascontiguousarray(ropeS)))
    return _CONST_CACHE


def _rep(v):
    return np.ascontiguousarray(np.broadcast_to(np.asarray(v, np.float32).reshape(1, -1), (128, np.asarray(v).size)))


def _pk(v):
    v = np.asarray(v, np.float32).reshape(-1, 128)
    return np.ascontiguousarray(v.T)


def kernel(x, c, ctx, c_ctx, w_ada, b_ada, norm_mix_w, norm_ffn_w, w_in, hgrn_lb_logits,
           hgrn_norm_w, q_norm_w, k_norm_w, attn_sinks, w_branch_hgrn, w_branch_attn,
           w_out, w_ffn_gate, w_ffn_up, w_ffn_down):
    f = lambda a: np.ascontiguousarray(np.asarray(a, np.float32))
    x = f(x); c = f(c); ctx = f(ctx); c_ctx = f(c_ctx)
    nc = bass.Bass("TRN2", target_bir_lowering=False)
    build(nc)
    cs = _consts()
    b_ada0 = f(b_ada)[0]
    lb = f(hgrn_lb_logits)
    shared = dict(
        w_ada=f(w_ada)[0], badaF=_pk(b_ada0),
        badaG=np.concatenate([_rep(b_ada0[2048:3072]), _rep(b_ada0[5120:6144])], axis=1),
        nwF=np.concatenate([_pk(f(norm_mix_w)[0]), _pk(f(norm_ffn_w)[0])], axis=1),
        w_in=f(w_in)[0],
        lbl=np.concatenate([_rep(lb[0, 0]), _rep(lb[0, 1]), _rep(lb[1, 0]), _rep(lb[1, 1])], axis=1),
        hnw=f(hgrn_norm_w)[0].reshape(128, 1).copy(),
        qkw=np.concatenate([_rep(np.tile(f(q_norm_w)[0], 8)), _rep(np.tile(f(k_norm_w)[0], 2))], axis=1),
        sinks=_rep(f(attn_sinks)[0]),
        w_bh=f(w_branch_hgrn)[0], w_ba=f(w_branch_attn)[0], w_o=f(w_out)[0],
        w_g=f(w_ffn_gate)[0], w_u=f(w_ffn_up)[0], w_d=f(w_ffn_down)[0],
        **cs)
    in_maps = []
    for b in range(8):
        m = dict(shared)
        m['x'] = x[b]
        m['ctx'] = ctx[b]
        m['c2'] = np.ascontiguousarray(np.stack([_pk(c[b]), _pk(c_ctx)], axis=2).reshape(128, 16))
        in_maps.append(m)
    res = run_bass_kernel_spmd(nc, in_maps, core_ids=list(range(8)))
    return np.stack([np.asarray(r["out"], np.float32) for r in res.results], axis=0)
```
